# Optimizing a Trainium2 kernel written in Bass

```python
import jax, jax.numpy as jnp
from jax import lax
import numpy as np

D_MODEL = 1024
BATCH = 8
SEQ = 2048
DEPTH = 2

HEAD_DIM = 64
D_MIX = D_MODEL
D_A = (3 * D_MIX) // 8
D_B = (D_MIX - D_A) // 2
D_C = D_MIX - D_A - D_B
H_A = D_A // HEAD_DIM
H_B = D_B // HEAD_DIM
H_C = D_C // HEAD_DIM
DECAY_LORA = 64
AAA_LORA = 64
N_SHIFT = 3 * D_A + DECAY_LORA + AAA_LORA
N_IN = N_SHIFT + 3 * D_B + 3 * D_C + H_C + D_MIX
BLOCK = 128
NORM_EPS = 1e-6
GN_EPS = 64e-5

kernel_name = "hybrid_rwkv7_stickbreak_fox"


def rmsnorm(x, w):
    xf = x.astype(jnp.float32)
    xf = xf * lax.rsqrt(jnp.mean(xf * xf, axis=-1, keepdims=True) + NORM_EPS)
    return xf.astype(x.dtype) * w


def rwkv7_mix(p, mu, w0, w_up, a0, a_up, k_k, k_a, r_k, ln_w, ln_b):
    B, S, _ = p.shape
    dt = p.dtype
    prev = jnp.pad(p[:, :-1], ((0, 0), (1, 0), (0, 0)))
    xs = p + mu * (prev - p)
    r, k, v, wl, al = jnp.split(xs, [D_A, 2 * D_A, 3 * D_A, 3 * D_A + DECAY_LORA], axis=-1)
    w = -jax.nn.softplus(-(w0 + jnp.tanh(wl) @ w_up)) - 0.5
    decay = jnp.exp(-jnp.exp(w.astype(jnp.float32)))
    a = jax.nn.sigmoid(a0 + al @ a_up).astype(jnp.float32)
    heads = lambda t: t.astype(jnp.float32).reshape(B, S, H_A, HEAD_DIM)
    kk = heads(k * k_k)
    kk = kk / jnp.maximum(jnp.sqrt(jnp.sum(kk * kk, axis=-1, keepdims=True)), 1e-12)
    k = heads(k.astype(jnp.float32) * (1.0 + (a - 1.0) * k_a.astype(jnp.float32)))
    r, v, a, decay = heads(r), heads(v), heads(a), heads(decay)
    a_vec = -kk
    b_vec = kk * a

    def step(state, inp):
        r_t, w_t, k_t, v_t, a_t, b_t = inp
        sa = jnp.einsum('bhvk,bhk->bhv', state, a_t)
        state = (state * w_t[:, :, None, :] + sa[..., None] * b_t[:, :, None, :]
                 + v_t[..., None] * k_t[:, :, None, :])
        return state, jnp.einsum('bhvk,bhk->bhv', state, r_t)

    tm = lambda t: jnp.moveaxis(t, 1, 0)
    s0 = jnp.zeros((B, H_A, HEAD_DIM, HEAD_DIM), jnp.float32)
    _, y = lax.scan(step, s0, (tm(r), tm(decay), tm(k), tm(v), tm(a_vec), tm(b_vec)))
    y = jnp.moveaxis(y, 0, 1)
    mean = jnp.mean(y, axis=-1, keepdims=True)
    var = jnp.mean(jnp.square(y - mean), axis=-1, keepdims=True)
    y = (y - mean) * lax.rsqrt(var + GN_EPS)
    y = y * ln_w.astype(jnp.float32).reshape(H_A, HEAD_DIM) + ln_b.astype(jnp.float32).reshape(H_A, HEAD_DIM)
    bonus = jnp.sum(r * k * r_k.astype(jnp.float32), axis=-1, keepdims=True) * v
    return (y + bonus).reshape(B, S, D_A).astype(dt)


def stick_breaking_attention(q, k, v):
    S = q.shape[1]
    scale = HEAD_DIM ** -0.5
    outs = []
    for i in range(S // BLOCK):
        s0, end = i * BLOCK, (i + 1) * BLOCK
        z = jnp.einsum('bqhd,bkhd->bhqk', q[:, s0:end], k[:, :end]).astype(jnp.float32) * scale
        t_idx = s0 + jnp.arange(BLOCK)[:, None]
        s_idx = jnp.arange(end)[None, :]
        strict = s_idx < t_idx
        log1m = jnp.where(strict, jax.nn.log_sigmoid(-z), 0.0)
        after = lax.cumsum(log1m, axis=3, reverse=True) - log1m
        attn = jnp.where(strict, jnp.exp(jax.nn.log_sigmoid(z) + after), 0.0)
        outs.append(jnp.einsum('bhqk,bkhd->bqhd', attn.astype(v.dtype), v[:, :end]))
    return jnp.concatenate(outs, axis=1)


def forgetting_attention(q, k, v, log_f):
    S = q.shape[1]
    scale = HEAD_DIM ** -0.5
    c = jnp.transpose(lax.cumsum(log_f, axis=1), (0, 2, 1))
    outs = []
    for i in range(S // BLOCK):
        s0, end = i * BLOCK, (i + 1) * BLOCK
        logits = jnp.einsum('bqhd,bkhd->bhqk', q[:, s0:end], k[:, :end]).astype(jnp.float32) * scale
        logits = logits + c[:, :, s0:end, None] - c[:, :, None, :end]
        causal = jnp.arange(end)[None, :] <= (s0 + jnp.arange(BLOCK)[:, None])
        probs = jax.nn.softmax(jnp.where(causal, logits, -jnp.inf), axis=-1)
        outs.append(jnp.einsum('bhqk,bkhd->bqhd', probs.astype(v.dtype), v[:, :end]))
    return jnp.concatenate(outs, axis=1)


def setup_inputs(seed: int = 0) -> dict:
    key = jax.random.key(seed)
    ks = jax.random.split(key, 17)
    nrm = jax.random.normal
    f32 = jnp.float32
    return {
        "x": nrm(ks[0], (BATCH, SEQ, D_MODEL), f32),
        "norm_w": 1.0 + 0.02 * nrm(ks[1], (DEPTH, D_MODEL), f32),
        "w_in": nrm(ks[2], (DEPTH, D_MODEL, N_IN), f32) * D_MODEL ** -0.5,
        "b_f": jax.random.uniform(ks[3], (DEPTH, H_C), f32, 1.0, 4.0),
        "mu": jax.random.uniform(ks[4], (DEPTH, N_SHIFT), f32),
        "w0": jax.random.uniform(ks[5], (DEPTH, D_A), f32, -6.0, 1.0),
        "w_up": 0.1 * nrm(ks[6], (DEPTH, DECAY_LORA, D_A), f32),
        "a0": 0.1 * nrm(ks[7], (DEPTH, D_A), f32),
        "a_up": 0.1 * nrm(ks[8], (DEPTH, AAA_LORA, D_A), f32),
        "k_k": 0.85 + 0.05 * nrm(ks[9], (DEPTH, D_A), f32),
        "k_a": 1.0 + 0.05 * nrm(ks[10], (DEPTH, D_A), f32),
        "r_k": 0.1 * nrm(ks[11], (DEPTH, H_A, HEAD_DIM), f32),
        "ln_x_w": 1.0 + 0.02 * nrm(ks[12], (DEPTH, D_A), f32),
        "ln_x_b": 0.02 * nrm(ks[13], (DEPTH, D_A), f32),
        "w_out": nrm(ks[14], (DEPTH, D_MIX, D_MODEL), f32) * D_MIX ** -0.5,
        "final_norm_w": 1.0 + 0.02 * nrm(ks[15], (D_MODEL,), f32),
    }


def reference(x, norm_w, w_in, b_f, mu, w0, w_up, a0, a_up, k_k, k_a, r_k,
              ln_x_w, ln_x_b, w_out, final_norm_w):
    B, S, _ = x.shape
    splits = [N_SHIFT,
              N_SHIFT + D_B, N_SHIFT + 2 * D_B, N_SHIFT + 3 * D_B,
              N_SHIFT + 3 * D_B + D_C, N_SHIFT + 3 * D_B + 2 * D_C, N_SHIFT + 3 * D_B + 3 * D_C,
              N_SHIFT + 3 * D_B + 3 * D_C + H_C]
    for l in range(DEPTH):
        h = rmsnorm(x, norm_w[l])
        p = h @ w_in[l]
        p_a, q_b, k_b, v_b, q_c, k_c, v_c, f_c, gate = jnp.split(p, splits, axis=-1)
        y_a = rwkv7_mix(p_a, mu[l], w0[l], w_up[l], a0[l], a_up[l], k_k[l], k_a[l],
                        r_k[l], ln_x_w[l], ln_x_b[l])
        hb = lambda t, n: t.reshape(B, S, n, HEAD_DIM)
        y_b = stick_breaking_attention(hb(q_b, H_B), hb(k_b, H_B), hb(v_b, H_B)).reshape(B, S, D_B)
        log_f = jax.nn.log_sigmoid((f_c + b_f[l]).astype(jnp.float32))
        y_c = forgetting_attention(hb(q_c, H_C), hb(k_c, H_C), hb(v_c, H_C), log_f).reshape(B, S, D_C)
        y = jnp.concatenate([y_a, y_b, y_c], axis=-1) * jax.nn.silu(gate)
        x = x + y @ w_out[l]
    return rmsnorm(x, final_norm_w)
```

```python
import numpy as np
import concourse.bass as bass
import concourse.mybir as mybir
from concourse.bass_utils import run_bass_kernel_spmd
from contextlib import ExitStack

F32 = mybir.dt.float32
BF16 = mybir.dt.bfloat16
AF = mybir.ActivationFunctionType
ALU = mybir.AluOpType

ENGS = ("pe", "act", "dve", "pool", "sp")
NDSEM = 6


class V:
    __slots__ = ("t", "ap", "parts")

    def __init__(self, t, ap, parts):
        self.t, self.ap, self.parts = t, ap, parts

    def __getitem__(self, idx):
        return V(self.t, self.ap[idx], self.parts)

    def keys(self):
        return [(self.t.id, p) for p in self.parts]

    def v(self):
        return self


class Tile:
    _n = 0

    def __init__(self, ap, nparts=1, name=""):
        Tile._n += 1
        self.id = Tile._n
        self.ap = ap
        self.nparts = nparts
        self.name = name
        self.is_psum = False

    def __getitem__(self, idx):
        return V(self, self.ap[idx], tuple(range(self.nparts)))

    def v(self):
        return V(self, self.ap, tuple(range(self.nparts)))

    def part(self, *ps):
        return V(self, self.ap, tuple(ps))


class Op:
    __slots__ = ("eng", "fn", "waits", "sig", "sem", "val", "kind", "pre")


class Sched:
    def __init__(self, nc, stack):
        self.nc = nc
        self.stack = stack
        self.ops = {e: [] for e in ENGS}
        self.lastw = {}
        self.readers = {}
        self.sems = {e: stack.enter_context(nc.semaphore(f"s_{e}")) for e in ENGS}
        self.dsems = {q: [stack.enter_context(nc.semaphore(f"d_{q}{i}")) for i in range(NDSEM)]
                      for q in ("sp", "pool", "act")}
        self.dcount = {q: 0 for q in ("sp", "pool", "act")}
        self.dlast = {q: [None] * NDSEM for q in ("sp", "pool", "act")}
        self.same_engine_sync = True
        self.psum_ids = set()

    def sbuf(self, name, shape, dtype, nparts=1):
        t = self.stack.enter_context(self.nc.sbuf_tensor(name, list(shape), dtype))
        return Tile(t.ap(), nparts, name)

    def psum(self, name, shape, dtype=F32, nparts=1):
        t = self.stack.enter_context(self.nc.psum_tensor(name, list(shape), dtype))
        tl = Tile(t.ap(), nparts, name)
        tl.is_psum = True
        self.psum_ids.add(tl.id)
        return tl

    def dram(self, name, shape, dtype, kind, nparts=1):
        t = self.nc.dram_tensor(name, list(shape), dtype, kind=kind).ap()
        return Tile(t, nparts, name)

    def _add(self, eng, fn, reads, writes, kind="c", dmaq=None):
        op = Op()
        op.eng, op.fn, op.kind, op.sig, op.sem, op.val = eng, fn, kind, False, None, None
        op.pre = []
        deps = []
        rk = [k for v in reads for k in v.keys()]
        wk = [k for v in writes for k in v.keys()]
        for k in rk:
            w = self.lastw.get(k)
            if w is not None:
                deps.append(w)
            if k[0] in self.psum_ids:
                rd = self.readers.get(k)
                if rd:
                    for e2, o2 in rd[0].items():
                        if e2 != eng:
                            deps.append(o2)
        for k in wk:
            w = self.lastw.get(k)
            if w is not None:
                deps.append(w)
            rd = self.readers.get(k)
            if rd:
                deps.extend(rd[0].values())
                deps.extend(rd[1])
        fdeps = []
        seen_ids = set()
        for d in deps:
            if d is op:
                continue
            if d.kind == "c" and d.eng == eng and kind == "c":
                if eng == "pe" or not self.same_engine_sync:
                    continue
            if id(d) not in seen_ids:
                seen_ids.add(id(d))
                fdeps.append(d)
        op.waits = fdeps
        if kind == "d":
            q = dmaq
            n = self.dcount[q]
            self.dcount[q] += 1
            slot = n % NDSEM
            prev = self.dlast[q][slot]
            if prev is not None:
                op.waits.append(prev)
            self.dlast[q][slot] = op
            op.sem = self.dsems[q][slot]
            op.val = 16 * (n // NDSEM + 1)
            op.sig = True
        for d in fdeps:
            d.sig = True
        wks = set(wk)
        for k in wk:
            self.lastw[k] = op
            self.readers[k] = None
        for k in rk:
            if k not in wks:
                rd = self.readers.get(k)
                if rd is None:
                    rd = self.readers[k] = ({}, [])
                if kind == "d":
                    rd[1].append(op)
                else:
                    rd[0][eng] = op
        self.ops[eng].append(op)
        return op

    def mm(self, out, lhsT, rhs, start=True, stop=True, extra_reads=(), **kw):
        nc = self.nc
        return self._add("pe", lambda: nc.tensor.matmul(out.ap, lhsT.ap, rhs.ap, start=start, stop=stop, **kw),
                         [lhsT, rhs, *extra_reads], [out])

    def transpose(self, out, in_, ident):
        nc = self.nc
        return self._add("pe", lambda: nc.tensor.transpose(out.ap, in_.ap, ident.ap), [in_, ident], [out])

    def act(self, out, in_, func, bias=None, scale=None, accum_out=None, eng="act"):
        nc = self.nc
        reads = [in_]
        kw = {}
        if bias is not None:
            if isinstance(bias, V):
                reads.append(bias)
                kw["bias"] = bias.ap
            else:
                kw["bias"] = bias
        if scale is not None:
            if isinstance(scale, V):
                reads.append(scale)
                kw["scale"] = scale.ap
            else:
                kw["scale"] = scale
        writes = [out]
        if accum_out is not None:
            kw["accum_out"] = accum_out.ap
            writes.append(accum_out)
        return self._add("act", lambda: nc.scalar.activation(out.ap, in_.ap, func, **kw), reads, writes)

    def _e(self, eng):
        return {"dve": self.nc.vector, "pool": self.nc.gpsimd, "act": self.nc.scalar}[eng]

    def tt(self, out, in0, in1, op, eng="dve"):
        e = self._e(eng)
        return self._add(eng, lambda: e.tensor_tensor(out.ap, in0.ap, in1.ap, op), [in0, in1], [out])

    def ts(self, out, in0, s1, op0, s2=None, op1=None, eng="dve", accum_out=None):
        e = self._e(eng)
        reads = [in0]
        a1 = s1.ap if isinstance(s1, V) else s1
        a2 = s2.ap if isinstance(s2, V) else s2
        if isinstance(s1, V):
            reads.append(s1)
        if isinstance(s2, V):
            reads.append(s2)
        kw = {}
        writes = [out]
        if op1 is not None:
            kw["op1"] = op1
        if accum_out is not None:
            kw["accum_out"] = accum_out.ap
            writes.append(accum_out)
        return self._add(eng, lambda: e.tensor_scalar(out.ap, in0.ap, a1, a2, op0, **kw), reads, writes)

    def stt(self, out, in0, scalar, in1, op0, op1, eng="dve"):
        e = self._e(eng)
        reads = [in0, in1]
        a = scalar.ap if isinstance(scalar, V) else scalar
        if isinstance(scalar, V):
            reads.append(scalar)
        return self._add(eng, lambda: e.scalar_tensor_tensor(out.ap, in0.ap, a, in1.ap, op0, op1), reads, [out])

    def scan(self, out, d0, d1, initial, op0, op1):
        nc = self.nc
        reads = [d0, d1]
        a = initial.ap if isinstance(initial, V) else initial
        if isinstance(initial, V):
            reads.append(initial)
        return self._add("dve", lambda: nc.vector.tensor_tensor_scan(out.ap, d0.ap, d1.ap, a, op0, op1), reads, [out])

    def copy(self, out, in_, eng="dve"):
        if eng == "act":
            return self.act(out, in_, AF.Identity)
        e = self._e(eng)
        return self._add(eng, lambda: e.tensor_copy(out.ap, in_.ap), [in_], [out])

    def memset(self, out, val, eng="dve"):
        e = self._e(eng)
        return self._add(eng, lambda: e.memset(out.ap, val), [], [out])

    def recip(self, out, in_):
        nc = self.nc
        return self._add("dve", lambda: nc.vector.reciprocal(out.ap, in_.ap), [in_], [out])

    def dma(self, out, in_, q="sp", **kw):
        e = {"sp": self.nc.sync, "pool": self.nc.gpsimd, "act": self.nc.scalar}[q]
        return self._add(q, lambda: e.dma_start(out.ap, in_.ap, **kw), [in_], [out], kind="d", dmaq=q)

    def emit(self, final_waits=()):
        nc = self.nc
        for e in ENGS:
            c = 0
            for op in self.ops[e]:
                if op.kind == "c" and op.sig:
                    c += 1
                    op.sem = self.sems[e]
                    op.val = c
        engobj = {"pe": nc.tensor, "act": nc.scalar, "dve": nc.vector, "pool": nc.gpsimd, "sp": nc.sync}
        finals = list(final_waits)

        def run(e):
            eo = engobj[e]
            seen = {}
            nw = 0
            for op in self.ops[e]:
                for d in op.waits:
                    key = id(d.sem)
                    if seen.get(key, 0) >= d.val:
                        continue
                    seen[key] = d.val
                    eo.wait_ge(d.sem, d.val)
                    nw += 1
                inst = op.fn()
                if op.sig:
                    inst.then_inc(op.sem, 16 if op.kind == "d" else 1)
            if e == "sp":
                for d in finals:
                    eo.wait_ge(d.sem, d.val)
            return nw

        with nc.Block() as block:
            @block.tensor
            def _(x):
                run("pe")

            @block.scalar
            def _(x):
                run("act")

            @block.vector
            def _(x):
                run("dve")

            @block.gpsimd
            def _(x):
                run("pool")

            @block.sync
            def _(x):
                run("sp")

    def check(self):
        for e in ENGS:
            c = 0
            for op in self.ops[e]:
                if op.kind == "c" and op.sig:
                    c += 1
                    op.sem = self.sems[e]
                    op.val = c
        semv = {}
        ptr = {e: 0 for e in ENGS}
        progress = True
        while progress:
            progress = False
            for e in ENGS:
                while ptr[e] < len(self.ops[e]):
                    op = self.ops[e][ptr[e]]
                    if all(semv.get(id(d.sem), 0) >= d.val for d in op.waits):
                        if op.sig:
                            k = id(op.sem)
                            semv[k] = semv.get(k, 0) + (16 if op.kind == "d" else 1)
                            if op.kind == "c":
                                assert semv[k] == op.val, (e, ptr[e], semv[k], op.val)
                        ptr[e] += 1
                        progress = True
                    else:
                        break
        stuck = {e: (ptr[e], len(self.ops[e])) for e in ENGS if ptr[e] < len(self.ops[e])}
        return stuck

    def stats(self):
        return {e: len(self.ops[e]) for e in ENGS}


def _v_bc(self, shape):
    return V(self.t, self.ap.broadcast_to(list(shape)), self.parts)


V.bc = _v_bc


S = 2048
DM = 1024
OFF_QB, OFF_KB, OFF_VB = 1280, 1600, 1920
OFF_QC, OFF_KC, OFF_VC = 2240, 2560, 2880
OFF_F, OFF_G = 3200, 3205
EPS = 1e-6
GN_EPS = 64e-5


def _groups():
    g = []
    g.append(("lora", list(range(1152, 1280))))
    for p in range(3):
        g.append((f"r{p}", list(range(128 * p, 128 * p + 128))))
        g.append((f"k{p}", list(range(384 + 128 * p, 384 + 128 * p + 128))))
        g.append((f"v{p}", list(range(768 + 128 * p, 768 + 128 * p + 128))))
    for nm, oq, ok_, ov in (("b", OFF_QB, OFF_KB, OFF_VB),):
        g.append(("qb0", list(range(oq, oq + 128)))); g.append(("qb1", list(range(oq + 128, oq + 256)))); g.append(("qb2", list(range(oq + 256, oq + 320))))
        g.append(("kb0", list(range(ok_, ok_ + 128)))); g.append(("kb1", list(range(ok_ + 128, ok_ + 256)))); g.append(("kb2", list(range(ok_ + 256, ok_ + 320))))
        g.append(("vb0", list(range(ov, ov + 128)))); g.append(("vb1", list(range(ov + 128, ov + 256)))); g.append(("vb2", list(range(ov + 256, ov + 320))))
    oq, ok_, ov = OFF_QC, OFF_KC, OFF_VC
    g.append(("qc0", list(range(oq, oq + 64)))); g.append(("qc1", list(range(oq + 64, oq + 192)))); g.append(("qc2", list(range(oq + 192, oq + 320))))
    g.append(("kc0", list(range(ok_, ok_ + 64)))); g.append(("kc1", list(range(ok_ + 64, ok_ + 192)))); g.append(("kc2", list(range(ok_ + 192, ok_ + 320))))
    g.append(("vc0", list(range(ov, ov + 64)))); g.append(("vc1", list(range(ov + 64, ov + 192)))); g.append(("vc2", list(range(ov + 192, ov + 320))))
    g.append(("fA", [OFF_F + j // 32 for j in range(96)]))
    g.append(("fB", [OFF_F + 3 + j // 32 for j in range(64)]))
    for m in range(8):
        g.append((f"g{m}", list(range(OFF_G + 128 * m, OFF_G + 128 * m + 128))))
    return g


GROUPS = _groups()
GOFF = {}
_o = 0
for _n, _c in GROUPS:
    GOFF[_n] = (_o, len(_c))
    _o += 8 * len(_c)
GOFF["wo"] = (_o, 1024)
_o += 8 * 1024
WTOT = _o

PC = {}
_pc = 0
for _l in range(2):
    for _nm in ("w0", "a0", "k_k", "k_a", "r_k", "ln_w", "ln_b"):
        for _p in range(3):
            PC[(_l, _nm, _p)] = _pc; _pc += 1
    PC[(_l, "bfA")] = _pc; _pc += 1
    PC[(_l, "bfB")] = _pc; _pc += 1
    for _k in range(8):
        PC[(_l, "nw", _k)] = _pc; _pc += 1
for _k in range(8):
    PC[("fnw", _k)] = _pc; _pc += 1
NPC = _pc

CC = {"ident": 0, "strict_ts": 128, "incl_ts": 256, "strict_st": 384, "incl_st": 512, "blk": 640, "ones": 768, "sel": 896,
      "nm_strict_ts": 899, "nm_incl_st": 1027, "nones": 1155, "zero": 1283}
NCONST = 1284
NEGBIG = -30000.0


def _host_consts():
    c = np.zeros((128, NCONST), np.float32)
    i = np.arange(128)
    c[:, 0:128] = np.eye(128)
    c[:, 128:256] = (i[None, :] < i[:, None])
    c[:, 256:384] = (i[None, :] <= i[:, None])
    c[:, 384:512] = (i[:, None] < i[None, :])
    c[:, 512:640] = (i[:, None] <= i[None, :])
    c[:, 640:768] = (i[:, None] // 64 == i[None, :] // 64)
    c[:, 768:896] = 1.0
    for r in range(3):
        c[:, 896 + r] = (i % 32 == r)
    c[:, 899:1027] = np.where(i[None, :] < i[:, None], 0.0, NEGBIG)
    c[:, 1027:1155] = np.where(i[:, None] <= i[None, :], 0.0, NEGBIG)
    c[:, 1155:1283] = -1.0
    return c


def _host_pack(inp):
    wpack = np.empty((2, 128, WTOT), np.float32)
    for l in range(2):
        W = inp["w_in"][l]
        for n, cols in GROUPS:
            o, L = GOFF[n]
            a = W[:, cols].reshape(8, 128, L).transpose(1, 0, 2).reshape(128, 8 * L)
            wpack[l, :, o:o + 8 * L] = a
        Wo = inp["w_out"][l]
        o, L = GOFF["wo"]
        wpack[l, :, o:o + 8 * L] = Wo.reshape(8, 128, 1024).transpose(1, 0, 2).reshape(128, 8 * 1024)
    pcol = np.zeros((128, NPC), np.float32)
    src = {"w0": "w0", "a0": "a0", "k_k": "k_k", "k_a": "k_a", "ln_w": "ln_x_w", "ln_b": "ln_x_b"}
    for l in range(2):
        for nm in ("w0", "a0", "k_k", "k_a", "r_k", "ln_w", "ln_b"):
            v = inp["r_k"][l].reshape(384) if nm == "r_k" else inp[src[nm]][l]
            for p in range(3):
                pcol[:, PC[(l, nm, p)]] = v[128 * p:128 * p + 128]
        bf = inp["b_f"][l]
        pcol[0:96, PC[(l, "bfA")]] = bf[np.arange(96) // 32]
        pcol[0:64, PC[(l, "bfB")]] = bf[3 + np.arange(64) // 32]
        for k in range(8):
            pcol[:, PC[(l, "nw", k)]] = inp["norm_w"][l][128 * k:128 * k + 128]
    for k in range(8):
        pcol[:, PC[("fnw", k)]] = inp["final_norm_w"][128 * k:128 * k + 128]
    lora = np.stack([np.stack([inp["w_up"][l], inp["a_up"][l]]) for l in range(2)]).astype(np.float32)
    return wpack, pcol, lora


def build(debug=None, nlayers=2, parts=("a", "b", "c")):
    nc = bass.Bass("TRN2", target_bir_lowering=False)
    st = ExitStack()
    with st:
        s = Sched(nc, st)
        xT = s.dram("xT", [DM, S], F32, "ExternalInput", nparts=8)
        wpack = s.dram("wpack", [2, 128, WTOT], F32, "ExternalInput")
        pcol_d = s.dram("pcol", [128, NPC], F32, "ExternalInput")
        mu_d = s.dram("mu", [2, 1280], F32, "ExternalInput")
        lora_d = s.dram("lora", [2, 2, 64, 384], F32, "ExternalInput")
        const_d = s.dram("consts", [128, NCONST], F32, "ExternalInput")
        xs1 = s.dram("xs1", [DM, S], F32, "Internal", nparts=8)
        xs2 = s.dram("xs2", [DM, S], F32, "Internal", nparts=8)
        outT = s.dram("outT", [DM, S], F32, "ExternalOutput", nparts=8)
        dbg = s.dram("dbg", [128, 8 * S], BF16, "ExternalOutput") if debug else None

        hT = s.sbuf("hT", [128, 8, S + 1], BF16)
        yT = s.sbuf("yT", [128, 8, S], BF16, nparts=8)
        WR = [s.sbuf(f"wr{i}", [128, 8 * 320], BF16) for i in range(2)]
        stage2 = s.sbuf("stage2", [128, 8, 128], F32)
        murep = s.sbuf("murep", [128, 1280], F32)
        cst = s.sbuf("cst", [128, NCONST], F32)
        cbf = s.sbuf("cbf", [128, NCONST], BF16)
        pcol = s.sbuf("pcolsb", [128, NPC], F32)
        pder = s.sbuf("pder", [128, 16], F32)
        lorabf = s.sbuf("lorabf", [64, 2, 384], BF16)
        FP = [s.sbuf(f"fp{i}", [128, 2052], F32, nparts=8) for i in range(4)]

        def _shv(t, c0, parts):
            return V(t, t.ap[:, c0:c0 + 1024].bitcast(BF16).rearrange("p (a k n) -> p a k n", a=2, k=8), parts)
        SH = [_shv(FP[2], 0, (0, 1, 2, 3)), _shv(FP[2], 1024, (4, 5, 6, 7)), _shv(FP[3], 0, (0, 1, 2, 3))]
        stage = V(FP[3], FP[3].ap[:, 1024:2048].rearrange("p (k n) -> p k n", k=8), (4, 5, 6, 7))
        BP = [s.sbuf(f"bp{i}", [128, 2048], BF16, nparts=4) for i in range(9)]
        ART = s.sbuf("art", [128, 16, 256], BF16)
        ART_flat = V(ART, ART.ap.rearrange("p c n -> p (c n)")[:, 0:2048], (0,))
        GC = s.sbuf("gc", [128, 16], F32)
        Hf = s.sbuf("Hf", [128, 64], F32)
        Hbz = s.sbuf("Hbz", [128, 2, 64], BF16)
        small = s.sbuf("small", [128, 64], F32)
        NT = [s.sbuf(f"negtot{i}", [128, 1], F32) for i in range(2)]
        EPS24 = s.sbuf("eps24", [128, 1], F32)
        GNEPS = s.sbuf("gneps", [128, 1], F32)
        NEPS = s.sbuf("neps", [128, 1], F32)
        CR = [s.sbuf(f"carry{i}", [128, 1], F32) for i in range(2)]
        NCT = s.sbuf("nctm", [128, 16, 8], F32)
        CT = s.sbuf("ctm", [128, 16, 8], F32)
        CSETS = [dict(Pm=[s.sbuf(f"Pm{j}{i}", [128, 2, 128], BF16) for i in range(2)],
                      QTm=[s.sbuf(f"QTm{j}{i}", [128, 2, 256], BF16) for i in range(2)],
                      AKK=s.sbuf(f"akk{j}", [128, 2, 256], BF16),
                      ARB=s.sbuf(f"arb{j}", [128, 2, 128], BF16))
                 for j in range(6)]
        Xb = s.sbuf("Xb", [128, 2, 64], BF16)
        Ub = s.sbuf("Ub", [128, 2, 64], BF16)

        PS = s.psum("PS", [128, 4096], F32, nparts=8)

        def bank(b):
            return V(PS, PS.ap[:, 512 * b:512 * b + 512], (b,))

        TP = V(PS, PS.ap[:, 3072:3584].bitcast(BF16), (6,))
        P5, P6, P7 = bank(7), bank(4), bank(5)
        NEGC = [BP[7], BP[8]]

        def zb(b):
            return bank(b)

        def zr(W, base=0):
            return V(PS, PS.ap[:, 512 * base:512 * base + W], tuple(range(base, base + (W + 511) // 512)))

        def fq(i, q, n=1, w=256):
            return V(FP[i], FP[i].ap[:, w * q:w * (q + n)], tuple(range(q * w // 256, (q + n) * w // 256)))

        def bq(i, q, w=512):
            return V(BP[i], BP[i].ap[:, w * q:w * q + w], (q,))

        def cc(name, n=128, rows=slice(0, 128)):
            return cst[rows, CC[name]:CC[name] + n]

        def cb(name, n=128, rows=slice(0, 128)):
            return cbf[rows, CC[name]:CC[name] + n]

        def pc(key, rows=slice(0, 128)):
            return pcol[rows, PC[key]:PC[key] + 1]

        s.dma(cst.v(), const_d.v())
        s.dma(cbf.v(), const_d.v(), q="pool")
        s.dma(pcol.v(), pcol_d.v())
        s.memset(hT[:, :, 0:1], 0.0, eng="pool")
        s.memset(EPS24.v(), 1e-24)
        s.memset(GNEPS.v(), GN_EPS)
        s.memset(NEPS.v(), EPS)
        if debug:
            s.memset(yT.v(), 0.0, eng="pool")

        wr_i = [0]

        def load_w(l, name):
            o, L = GOFF[name]
            t = WR[wr_i[0] % 2]
            wr_i[0] += 1
            s.dma(t[:, 0:8 * L], wpack[l, :, o:o + 8 * L], q="pool")
            return V(t, t.ap[:, 0:8 * L].rearrange("p (k n) -> p k n", k=8), (0,)), L

        sh_i = [0]

        def load_shift(l, name, mu0):
            o, L = GOFF[name]
            t = SH[sh_i[0] % 3]
            sh_i[0] += 1
            s.dma(stage, V(wpack, wpack.ap[l, :, o:o + 8 * L].rearrange("p (k n) -> p k n", k=8), (0,)))
            mub = V(murep, murep.ap[:, mu0:mu0 + 128].rearrange("p (o n) -> p o n", o=1).broadcast_to([128, 8, 128]), (0,))
            s.tt(stage2.v(), stage, mub, ALU.mult, eng="pool")
            s.copy(t[:, 1], stage2.v(), eng="pool")
            s.tt(t[:, 0], stage, stage2.v(), ALU.subtract, eng="pool")
            return t

        WSH = {}

        def get_shift(l, name, mu0):
            if (l, name) not in WSH:
                WSH[(l, name)] = load_shift(l, name, mu0)
            return WSH[(l, name)]

        def prefetch_layer(l, full):
            s.dma(murep.v(), V(mu_d, mu_d.ap[l].partition_broadcast(128), (0,)))
            s.dma(lorabf.v(), V(lora_d, lora_d.ap[l].rearrange("a j c -> j a c"), (0,)), q="pool")
            get_shift(l, "lora", 1152)
            get_shift(l, "r0", 0)
            get_shift(l, "k0", 384)

        def proj_fm(out, wv, L, t0, n, wv2=None):
            last = 7
            for k in range(8):
                s.mm(out, wv[:, k, 0:L], hT[:, k, 1 + t0:1 + t0 + n], start=(k == 0), stop=(k == last and wv2 is None))
            if wv2 is not None:
                for k in range(8):
                    s.mm(out, wv2[:, k, 0:L], hT[:, k, t0:t0 + n], start=False, stop=(k == last))

        def xset(sp):
            if sp % 2 == 0:
                return [(V(FP[i], FP[i].ap[:, 0:2048].rearrange("p (k t) -> p k t", k=4), tuple(range(8))), 4 * i, 4) for i in range(2)]
            artf = V(ART, ART.ap.rearrange("p c n -> p (c n)").bitcast(F32).rearrange("p (k t) -> p k t", k=4), (0,))
            b4 = V(BP[4], BP[4].ap.bitcast(F32).rearrange("p (k t) -> p k t", k=2), (0, 1, 2, 3))
            b5 = V(BP[5], BP[5].ap.bitcast(F32).rearrange("p (k t) -> p k t", k=2), (0, 1, 2, 3))
            return [(artf, 0, 4), (b4, 4, 2), (b5, 6, 2)]

        def xk(pieces, k):
            for v_, k0, nk in pieces:
                if k0 <= k < k0 + nk:
                    return v_[:, k - k0, :]

        def norm_phase(src, nwkeys, to_h, dst=None):
            srcv = src.ap.rearrange("(k p) t -> p k t", p=128)
            fin = []

            def issue(sp_):
                for v_, k0, nk in xset(sp_):
                    s.dma(v_, V(src, srcv[:, k0:k0 + nk, 512 * sp_:512 * sp_ + 512], tuple(range(k0, k0 + nk))), q="act")
            issue(0)
            issue(1)
            for sp in range(4):
                t0 = 512 * sp
                pcs = xset(sp)
                sq = [V(BP[7 + i], BP[7 + i].ap.rearrange("p (k t) -> p k t", k=4), (0, 1, 2, 3)) for i in range(2)]
                for v_, k0, nk in pcs:
                    s.act(sq[k0 // 4][:, k0 % 4:k0 % 4 + nk, :], v_, AF.Square)
                if 1 <= sp < 3:
                    issue(sp + 1)
                for k in range(8):
                    s.mm(P5.v(), cb("ones"), sq[k // 4][:, k % 4, :], start=(k == 0), stop=(k == 7))
                s.act(P6.v(), P5.v(), AF.Ln, bias=NEPS.v(), scale=1.0 / DM)
                s.act(P7.v(), P6.v(), AF.Exp, scale=-0.5)
                for k in range(8):
                    s.stt(hT[:, k, 1 + t0:1 + t0 + 512], xk(pcs, k), pc(nwkeys(k)), P7.v(), ALU.mult, ALU.mult)
            return fin

        def gates_all(l):
            for m in range(8):
                wv, L = load_w(l, f"g{m}")
                for sp in range(4):
                    t0 = 512 * sp
                    ps = zb((4 * m + sp) % 4)
                    proj_fm(ps, wv, 128, t0, 512)
                    s.act(V(yT, yT.ap[:, m, t0:t0 + 512], (m,)), ps, AF.Silu)

        def gate_and_store(l, m):
            wv, L = load_w(l, f"g{m}")
            for sp in range(4):
                t0 = 512 * sp
                ps = zb(sp % 4)
                proj_fm(ps, wv, 128, t0, 512)
                g = fq(0, 2 * (sp % 2), 2)
                s.act(g, ps, AF.Silu)
                yv = V(yT, yT.ap[:, m, t0:t0 + 512], (m,))
                s.tt(yv, yv, g, ALU.mult)

        def rwkv(l):
            TW, AL, BT, KT, BTM, KTM, VTM, BS, BON = BP
            for p in range(3):
                s.ts(pder[:, p:p + 1], pc((l, "w0", p)), -1.0, ALU.mult)
                s.ts(pder[:, 3 + p:4 + p], pc((l, "k_a", p)), -1.0, ALU.mult, 1.0, ALU.add)
                s.ts(pder[:, 6 + p:7 + p], pc((l, "a0", p)), -1.0, ALU.mult)
            shl = get_shift(l, "lora", 1152)
            for sp in range(4):
                t0 = 512 * sp
                ps = zb(sp % 4)
                proj_fm(ps, shl[:, 0], 128, t0, 512, shl[:, 1])
                s.act(TW[0:64, t0:t0 + 512], ps[0:64, :], AF.Tanh)
                s.copy(AL[0:64, t0:t0 + 512], ps[64:128, :], eng="dve")
            import os
            RW = int(os.environ.get("RW_STOP", "99"))
            if RW <= 1:
                return
            for p in range(3 if RW > 4 else 1):
                shr = get_shift(l, f"r{p}", 128 * p)
                shk = get_shift(l, f"k{p}", 384 + 128 * p)
                shv = get_shift(l, f"v{p}", 768 + 128 * p)
                negw0 = pder[:, p:p + 1]
                omka = pder[:, 3 + p:4 + p]
                N = 256
                nega0 = pder[:, 6 + p:7 + p]
                t_r, t_k, t_v, t_e, t_c, t_a = [fq(0, i) for i in range(6)]
                t_kk, t_sq = fq(0, 6), fq(0, 7)
                t_ri, t_kp, t_b, t_x = fq(1, 0), fq(1, 1), fq(1, 2), fq(1, 3)
                r3 = lambda v: V(v.t, v.ap.rearrange("p (c t) -> p c t", t=128), v.parts)
                nch = N // 128

                def part1(sp, which):
                    t0 = N * sp
                    cs = slice(t0, t0 + N)
                    if which == 0:
                        proj_fm(zb(0)[:, 0:N], shr[:, 0], 128, t0, N, shr[:, 1])
                    elif which == 1:
                        proj_fm(zb(1)[:, 0:N], shk[:, 0], 128, t0, N, shk[:, 1])
                    else:
                        proj_fm(zb(2)[:, 0:N], shv[:, 0], 128, t0, N, shv[:, 1])
                        s.mm(zb(3)[:, 0:N], lorabf[:, 0, 128 * p:128 * p + 128], TW[0:64, cs])
                        s.mm(P5[:, 0:N], lorabf[:, 1, 128 * p:128 * p + 128], AL[0:64, cs])

                bfv = lambda v: V(v.t, v.ap.bitcast(BF16), v.parts)
                sq_rk = bfv(fq(1, 4))
                sqb, rkb = sq_rk[:, 0:N], sq_rk[:, N:2 * N]
                tbuf = lambda sp, q: bq(7, q)[:, N * (sp % 2):N * (sp % 2) + N]

                def part2a(sp):
                    pr, pk, pv, pw = zb(0)[:, 0:N], zb(1)[:, 0:N], zb(2)[:, 0:N], zb(3)[:, 0:N]
                    pa = P5[:, 0:N]
                    s.copy(t_r, pr, eng="dve")
                    s.act(t_k, pk, AF.Identity)
                    s.copy(t_v, pv, eng="dve")
                    s.act(tbuf(sp, 2), pv, AF.Identity)
                    s.act(t_e, pw, AF.Exp, bias=negw0, scale=-1.0)
                    s.act(t_a, pa, AF.Exp, bias=nega0, scale=-1.0)

                def part2b(sp, stage):
                    t0 = N * sp
                    cs = slice(t0, t0 + N)
                    pss, pbn = P6[:, 0:N], P7[:, 0:N]
                    if stage == 1:
                        s.act(t_ri, pss, AF.Ln, bias=EPS24.v())
                        s.act(t_ri, t_ri, AF.Exp, scale=-0.5)
                        s.tt(t_kk, t_kk, t_ri, ALU.mult)
                        s.tt(t_b, t_kk, t_a, ALU.mult, eng="pool")
                        s.ts(t_x, t_a, pc((l, "k_a", p)), ALU.mult, omka, ALU.add)
                        s.tt(t_kp, t_k, t_x, ALU.mult, eng="pool")
                        s.stt(rkb, t_r, pc((l, "r_k", p)), t_kp, ALU.mult, ALU.mult)
                        s.mm(pbn, cb("blk"), rkb)
                        return
                    if stage == 2:
                        s.tt(BON[:, cs], pbn, t_v, ALU.mult)
                        return
                    s.act(t_e, t_e, AF.Ln, bias=1.0)
                    s.act(t_e, t_e, AF.Exp, bias=-0.5, scale=-1.0)
                    s.act(t_a, t_a, AF.Ln, bias=1.0)
                    s.act(t_a, t_a, AF.Exp, scale=-1.0)
                    for ch in range(nch):
                        cc_ = slice(128 * ch, 128 * ch + 128)
                        s.scan(t_c[:, cc_], cc("ones"), t_e[:, cc_], 0.0, ALU.mult, ALU.subtract)
                    s.ts(t_kk, t_k, pc((l, "k_k", p)), ALU.mult)
                    s.tt(sqb, t_kk, t_kk, ALU.mult)
                    s.mm(pss, cb("blk"), sqb)

                def part3a(sp):
                    t0 = N * sp
                    cs = slice(t0, t0 + N)
                    s.tt(t_k, t_c, t_e, ALU.add, eng="pool")
                    s.act(t_k, t_k, AF.Exp)
                    s.act(t_a, t_c, AF.Exp, scale=-1.0)
                    s.act(t_sq, t_c, AF.Exp)
                    for ch in range(nch):
                        cc_ = slice(128 * ch, 128 * ch + 128)
                        s.act(t_ri[:, cc_], t_c[:, cc_], AF.Exp, bias=t_c[:, 128 * ch + 127:128 * ch + 128], scale=-1.0)
                    c0 = t0 // 128
                    s.copy(GC[:, c0:c0 + nch], t_sq[:, 127:N:128], eng="dve")
                    s.stt(ART[:, c0:c0 + nch, 0:128], r3(t_kk), -1.0, r3(t_k), ALU.mult, ALU.mult)
                    s.tt(ART[:, c0:c0 + nch, 128:256], r3(t_r), r3(t_sq), ALU.mult, eng="pool")
                    s.tt(BT[:, cs], t_b, t_a, ALU.mult, eng="pool")
                    s.tt(KT[:, cs], t_kp, t_a, ALU.mult, eng="pool")
                    s.tt(tbuf(sp, 0), t_b, t_ri, ALU.mult)
                    s.tt(tbuf(sp, 1), t_kp, t_ri, ALU.mult)

                def part3t(sp):
                    for j in range(3):
                        src_ = tbuf(sp, j)
                        for ch in range(nch):
                            s.transpose(TP[:, 128 * (2 * j + ch):128 * (2 * j + ch) + 128], src_[:, 128 * ch:128 * ch + 128], cb("ident"))

                def part3e(sp):
                    cs = slice(N * sp, N * sp + N)
                    for j, dstt in enumerate((BTM, KTM, VTM)):
                        s.copy(dstt[:, cs], TP[:, 256 * j:256 * j + N], eng=("dve" if j == 1 else "act"))

                nsp = 8
                for w_ in range(3):
                    part1(0, w_)
                for sp in range(nsp):
                    nx = sp + 1 < nsp
                    part2a(sp)
                    if nx:
                        part1(sp + 1, 0)
                    if sp > 0:
                        part3e(sp - 1)
                    part2b(sp, 0)
                    if nx:
                        part1(sp + 1, 1)
                    part2b(sp, 1)
                    if nx:
                        part1(sp + 1, 2)
                    part2b(sp, 2)
                    part3a(sp)
                    part3t(sp)
                part3e(nsp - 1)
                if p < 2:
                    get_shift(l, f"r{p + 1}", 128 * (p + 1))
                    get_shift(l, f"k{p + 1}", 384 + 128 * (p + 1))
                    get_shift(l, f"v{p + 1}", 768 + 128 * (p + 1))
                if RW <= 2:
                    continue
                s.memset(Hf.v(), 0.0)
                s.memset(Hbz.v(), 0.0)
                hs = [slice(0, 64), slice(64, 128)]
                bc3 = lambda name, n: V(cst, cst.ap[:, CC[name]:CC[name] + n], (0,))
                r2 = lambda v, n: V(v.t, v.ap.rearrange("p (e n) -> p e n", e=2), v.parts)
                idb = V(cbf, cbf.ap[:, 0:128].rearrange("p (o n) -> p o n", o=1).broadcast_to([128, 2, 128]), (0,))

                def chain(c, CS, bA, bB):
                    Pm, QTm, AKK, ARB, fin = CS["Pm"], CS["QTm"], CS["AKK"], CS["ARB"], CS
                    ccs = slice(128 * c, 128 * c + 128)
                    for e in range(2):
                        pP, pQ, pK = bank(bA)[:, 0:128], bank(bA)[:, 128:384], bank(bB)[:, 0:256]
                        s.mm(pP, ART[hs[e], c, 0:128], BT[hs[e], ccs]); yield
                        s.mm(pQ, BT[hs[e], ccs], ART[hs[e], c, :]); yield
                        s.mm(pK, KT[hs[e], ccs], ART[hs[e], c, :]); yield
                        s.tt(Pm[0][:, e, :], pP, bc3("strict_ts", 128), ALU.mult); yield
                        s.tt(QTm[0][:, e, 0:128], pQ[:, 0:128], bc3("strict_st", 128), ALU.mult); yield
                        s.tt(ARB[:, e, :], pQ[:, 128:256], bc3("incl_st", 128), ALU.mult); yield
                        s.tt(AKK[:, e, :], pK, bc3("strict_st", 256), ALU.mult); yield
                    s.act(QTm[0][:, :, 128:256], idb, AF.Identity); yield
                    cur = 0
                    for lev in range(6):
                        nxt = 1 - cur
                        pA, pB = bank(bA), bank(bB)[:, 0:256]
                        for e in range(2):
                            s.mm(pA[:, 256 * e:256 * e + 256], Pm[cur][:, e, :], QTm[cur][:, e, :]); yield
                            s.mm(pB[:, 128 * e:128 * e + 128], QTm[cur][:, e, 0:128], Pm[cur][:, e, :]); yield
                        pA3 = r2(pA, 256)
                        s.act(QTm[nxt][:, :, 0:128], pA3[:, :, 0:128], AF.Identity); yield
                        s.tt(QTm[nxt][:, :, 128:256], pA3[:, :, 128:256], QTm[cur][:, :, 128:256], ALU.add); yield
                        s.act(Pm[nxt].v(), r2(pB, 128), AF.Identity); yield
                        cur = nxt
                    pA = bank(bA)[:, 0:256]
                    for e in range(2):
                        s.mm(pA[:, 128 * e:128 * e + 128], Pm[cur][:, e, :], QTm[cur][:, e, 128:256]); yield
                    s.tt(QTm[1 - cur][:, :, 128:256], r2(pA, 128), QTm[cur][:, :, 128:256], ALU.add); yield
                    fin["TT"] = QTm[1 - cur]

                def serial(c, CS):
                    AKK, ARB, TTf = CS["AKK"], CS["ARB"], CS["TT"]
                    pX, pU = P5[:, 0:128], P5[:, 128:256]
                    for e in range(2):
                        s.mm(pX[:, 64 * e:64 * e + 64], ART[:, c, 0:128], Hbz[:, e, :], start=True, stop=False); yield
                        s.mm(pX[:, 64 * e:64 * e + 64], AKK[:, e, 0:128], VTM[:, 128 * c + 64 * e:128 * c + 64 * e + 64], start=False, stop=True); yield
                    s.copy(Xb.v(), r2(pX, 64), eng="dve"); yield
                    for e in range(2):
                        s.mm(pU[:, 64 * e:64 * e + 64], TTf[:, e, 128:256], Xb[:, e, :]); yield
                    s.act(Ub.v(), r2(pU, 64), AF.Identity); yield
                    pH = P5[:, 256:320]
                    for e in range(2):
                        s.mm(pH[hs[e], :], BTM[:, 128 * c + 64 * e:128 * c + 64 * e + 64], Ub[:, e, :], start=True, stop=False); yield
                        s.mm(pH[hs[e], :], KTM[:, 128 * c + 64 * e:128 * c + 64 * e + 64], VTM[:, 128 * c + 64 * e:128 * c + 64 * e + 64], start=False, stop=True); yield
                    pY = P7[:, 128 * (c % 4):128 * (c % 4) + 128]
                    for e in range(2):
                        s.mm(pY[hs[e], :], Hbz[:, e, :], ART[:, c, 128:256], start=True, stop=False); yield
                        s.mm(pY[hs[e], :], Ub[:, e, :], ARB[:, e, :], start=False, stop=False); yield
                        s.mm(pY[hs[e], :], VTM[:, 128 * c + 64 * e:128 * c + 64 * e + 64], AKK[:, e, 128:256], start=False, stop=True); yield
                    s.stt(Hf.v(), Hf.v(), GC[:, c:c + 1], pH, ALU.mult, ALU.add); yield
                    s.act(Hbz[0:64, 0, :], Hf[0:64, :], AF.Identity); yield
                    s.copy(Hbz[64:128, 1, :], Hf[64:128, :], eng="dve"); yield
                    if c % 4 == 3:
                        t0 = 512 * (c // 4)
                        cs5 = slice(t0, t0 + 512)
                        y = fq(1, 4, 2)
                        s.act(y, P7.v(), AF.Identity); yield
                        yb_, ysqb_ = bq(7, 0), bq(7, 1)
                        s.act(yb_, P7.v(), AF.Identity); yield
                        s.act(ysqb_, P7.v(), AF.Square); yield
                        pm_, pe_ = P5.v(), P5.v()
                        s.mm(pm_, cb("blk"), yb_); yield
                        mean = fq(0, 0, 2)
                        s.act(mean, pm_, AF.Identity, scale=1.0 / 64); yield
                        s.mm(pe_, cb("blk"), ysqb_); yield
                        var = fq(0, 2, 2)
                        s.tt(var, mean, mean, ALU.mult, eng="pool"); yield
                        s.stt(var, pe_, 1.0 / 64, var, ALU.mult, ALU.subtract); yield
                        s.act(var, var, AF.Ln, bias=GNEPS.v()); yield
                        s.act(var, var, AF.Exp, scale=-0.5); yield
                        s.tt(y, y, mean, ALU.subtract, eng="pool"); yield
                        s.tt(y, y, var, ALU.mult, eng="pool"); yield
                        s.ts(y, y, pc((l, "ln_w", p)), ALU.mult, pc((l, "ln_b", p)), ALU.add); yield
                        yv = V(yT, yT.ap[:, p, cs5], (p,))
                        s.tt(y, y, BON[:, cs5], ALU.add, eng="pool"); yield
                        s.tt(yv, y, yv, ALU.mult); yield

                def interleave(ga, gb, ka=3):
                    da = db = False
                    while not (da and db):
                        for _ in range(ka):
                            if not da:
                                try:
                                    next(ga)
                                except StopIteration:
                                    da = True
                        if not db:
                            try:
                                next(gb)
                            except StopIteration:
                                db = True

                def lockstep(gens):
                    gens = list(gens)
                    while gens:
                        for g_ in list(gens):
                            try:
                                next(g_)
                            except StopIteration:
                                gens.remove(g_)
                        yield

                def seq(gens):
                    for g_ in gens:
                        yield from g_

                CB = [(0, 1), (2, 3), (4, 6)]
                nchunks = 16 if RW > 3 else 1
                groups = [list(range(g0, min(g0 + 3, nchunks))) for g0 in range(0, nchunks, 3)]
                mk = lambda grp: lockstep([chain(c, CSETS[c % 6], *CB[c % 3]) for c in grp])
                for _ in mk(groups[0]):
                    pass
                for gi_, grp in enumerate(groups):
                    gb_ = seq([serial(c, CSETS[c % 6]) for c in grp])
                    if gi_ + 1 < len(groups):
                        interleave(mk(groups[gi_ + 1]), gb_, ka=1)
                    else:
                        for _ in gb_:
                            pass

        def attn_group(l, kind, gi):
            QT, KTt, A, AT, VT = BP[0], BP[1], BP[2], BP[3], BP[4]
            E_, F_ = FP[0], FP[1]
            if kind == "b":
                s.memset(E_[:, 0:1], 0.0)
            wq, Lq = load_w(l, f"q{kind}{gi}")
            if kind == "b":
                heads = [(2 * gi + e, 64 * e) for e in range(2 if gi < 2 else 1)]
            else:
                heads = [(0, 64)] if gi == 0 else [(2 * gi - 1 + e, 64 * e) for e in range(2)]
            rows = slice(64, 128) if (kind == "c" and gi == 0) else slice(0, Lq)
            for sp in range(4):
                ps = zb(sp)
                proj_fm(ps[rows, :], wq, Lq, 512 * sp, 512)
                s.act(QT[rows, 512 * sp:512 * sp + 512], ps[rows, :], AF.Identity, scale=0.125)
            wk, Lk = load_w(l, f"k{kind}{gi}")
            for sp in range(4):
                ps = zb(sp)
                proj_fm(ps[rows, :], wk, Lk, 512 * sp, 512)
                s.copy(KTt[rows, 512 * sp:512 * sp + 512], ps[rows, :], eng="dve")
            wv, Lv = load_w(l, f"v{kind}{gi}")
            for b in range(16):
                ps = zb(b % 4)[:, 0:Lv]
                for k in range(8):
                    s.mm(ps, hT[:, k, 1 + 128 * b:1 + 128 * b + 128], wv[:, k, 0:Lv], start=(k == 0), stop=(k == 7))
                if b % 2 == 0:
                    s.act(VT[:, 128 * b:128 * b + Lv], ps, AF.Identity)
                else:
                    s.copy(VT[:, 128 * b:128 * b + Lv], ps, eng="dve")
            sets = [(FP[0], FP[1], BP[2], BP[3]), (FP[2], FP[3], BP[5], BP[6])]
            if kind == "b":
                s.memset(FP[2][:, 0:1], 0.0)
            for hi, (h, pb) in enumerate(heads):
                g = (6 + h) if kind == "b" else (11 + h)
                m = g // 2
                assert pb == 64 * (g % 2)
                hr = slice(pb, pb + 64)
                vcol = 0 if (kind == "c" and gi == 0) else pb
                if kind == "b":
                    sb_head(sets, QT, KTt, VT, hr, vcol, m)
                else:
                    fox_head(QT, KTt, VT, hr, vcol, m, h)

        zctr = [0]

        def sb_head(sets, QT, KTt, VT, hr, vcol, m):
            A3 = [(BP[2], BP[3]), (BP[5], BP[6]), (BP[7], BP[8])]
            def S1(i):
                Qb, Sb, _a, _b = sets[i % 2]
                As, ATs = A3[i % 3]
                W = 128 * (i + 1)
                tq = slice(128 * i, 128 * i + 128)
                units = [(c, min(W, c + 1536)) for c in range(0, W, 1536)][::-1]
                fv = lambda t, a_, b_: V(t, t.ap[:, a_:b_], tuple(range(a_ // 256, (b_ - 1) // 256 + 1)))
                rv = lambda t, hi, lo: V(t, t.ap[:, hi:(lo if lo >= 0 else None):-1], tuple(range((lo + 1) // 256, hi // 256 + 1)))
                zss = []
                for (c0, c1) in units:
                    nb_ = (c1 - c0 + 511) // 512
                    if zctr[0] + nb_ > 6:
                        zctr[0] = 0
                    zs = zctr[0]
                    zctr[0] += nb_
                    zss.append(zs)
                    for b_ in range((c1 - c0 + 511) // 512):
                        k0 = c0 + 512 * b_
                        w_ = min(512, c1 - k0)
                        last = (k0 + w_ == W)
                        s.mm(bank(zs + b_)[:, 0:w_], QT[hr, tq], KTt[hr, k0:k0 + w_], start=True, stop=not last)
                        if last:
                            s.mm(bank(zs + b_)[:, w_ - 128:w_], cb("ident"), cb("nm_strict_ts"), start=False, stop=True)
                s.memset(fv(Qb, W, W + 1), 1.0, eng="pool")
                for ui, (c0, c1) in enumerate(units):
                    w = c1 - c0
                    zv = V(PS, PS.ap[:, 512 * zss[ui]:512 * zss[ui] + w], tuple(range(zss[ui], zss[ui] + (w + 511) // 512)))
                    s.act(fv(Qb, c0, c1), zv, AF.Sigmoid, scale=-1.0)
                    s.act(zv, zv, AF.Sigmoid)
                    init = 1.0 if ui == 0 else fv(Sb, c1, c1 + 1)
                    s.scan(rv(Sb, c1 - 1, c0 - 1), rv(Qb, c1, c0), cc("zero", 1).bc([128, w]), init, ALU.mult, ALU.add)
                    s.tt(As[:, c0:c1], zv, fv(Sb, c0, c1), ALU.mult)

            def S2T(i):
                As, ATs = A3[i % 3]
                for j in range(i + 1):
                    s.transpose(TP[:, 128 * (j % 8):128 * (j % 8) + 128], As[:, 128 * j:128 * j + 128], cb("ident"))
                    if j % 8 == 7 or j == i:
                        j0 = (j // 8) * 8
                        n_ = j - j0 + 1
                        s.act(ATs[:, 128 * j0:128 * j0 + 128 * n_], TP[:, 0:128 * n_], AF.Identity)

            def S2P(i):
                As, ATs = A3[i % 3]
                tq = slice(128 * i, 128 * i + 128)
                pO = P5[hr, 0:128]
                for j in range(i + 1):
                    s.mm(pO, VT[:, 128 * j + vcol:128 * j + vcol + 64], ATs[:, 128 * j:128 * j + 128], start=(j == 0), stop=(j == i))
                yv_ = V(yT, yT.ap[hr, m, tq], (m,))
                s.tt(yv_, pO, yv_, ALU.mult)

            for step in range(18):
                if step < 16:
                    S1(step)
                if 1 <= step < 17:
                    S2T(step - 1)
                if 2 <= step:
                    S2P(step - 2)

        def fox_head(QT, KTt, VT, hr, vcol, m, h):
            ncr = NEGC[0 if h < 3 else 1]
            nb = 32 * (h if h < 3 else h - 3)
            pb = hr.start
            ET, VA, QA, KA = BP[2], BP[3], BP[5], BP[6]
            s.copy(QA[0:64, :], QT[hr, :], eng="dve")
            s.act(KA[0:64, :], KTt[hr, :], AF.Identity)
            s.memset(QA[64:128, :], 0.0, eng="pool")
            s.memset(KA[64:128, :], 0.0, eng="pool")
            s.copy(QA[64:67, :], ncr[nb:nb + 3, :], eng="dve")
            s.memset(KA[64:67, :], -1.0)
            va3 = V(VA, VA.ap.rearrange("p (b c) -> p b c", c=128), (0, 1, 2, 3))
            vsrc = V(VT, VT.ap.rearrange("p (b c) -> p b c", c=128), (0, 1, 2, 3))
            s.copy(va3[:, :, pb:pb + 64], vsrc[:, :, vcol:vcol + 64], eng="dve")
            s.memset(va3[:, :, 64 - pb:128 - pb], 1.0)
            units = [(q, j) for q in range(2) for j in range(8 * q + 8)]
            dr = slice(64 - pb, 128 - pb)

            def halves(q, j):
                out_ = []
                for hh in range(2):
                    sp = 2 * q + hh
                    col0 = max(0, 128 * j - 512 * sp)
                    if col0 < 512:
                        out_.append((hh, sp, col0))
                return out_

            def A(u):
                q, j = units[u]
                zs = 2 * (u % 2)
                hv = halves(q, j)
                for hh, sp, col0 in hv:
                    t0 = 512 * sp
                    ps = bank(zs + hh)[:, col0:512]
                    diag = (128 * j >= t0)
                    s.mm(ps, KA[:, 128 * j:128 * j + 128], QA[:, t0 + col0:t0 + 512], start=True, stop=not diag)
                    if diag:
                        s.mm(bank(zs + hh)[:, col0:col0 + 128], cb("ident"), cb("nm_incl_st"), start=False, stop=True)
                h0, _sp0, c0 = hv[0]
                g0 = 512 * h0 + c0
                zin = V(PS, PS.ap[:, 512 * zs + g0:512 * zs + 1024], tuple(range(zs + h0, zs + 2)))
                e_ = u % 2
                eout = V(BP[2], BP[2].ap[:, 1024 * e_ + g0:1024 * e_ + 1024], tuple(range(2 * e_ + h0, 2 * e_ + 2)))
                s.act(eout, zin, AF.Exp, bias=NCT[:, j, h:h + 1])

            def B(u):
                q, j = units[u]
                e_ = u % 2
                for hh, sp, col0 in halves(q, j):
                    bk = 4 + 2 * (q % 2) + hh
                    et = V(BP[2], BP[2].ap[:, 1024 * e_ + 512 * hh + col0:1024 * e_ + 512 * hh + 512], (2 * e_ + hh,))
                    last = (j == 4 * sp + 3)
                    s.mm(bank(bk)[:, col0:512], va3[:, j, :], et, start=(j == 0), stop=last)
                    if last:
                        rc = V(FP[1], FP[1].ap[hr, 512 * hh:512 * hh + 512], tuple(range(8)))
                        s.recip(rc, bank(bk)[dr, :])
                        yv_ = V(yT, yT.ap[hr, m, 512 * sp:512 * sp + 512], (m,))
                        s.tt(rc, rc, yv_, ALU.mult)
                        s.tt(yv_, bank(bk)[hr, :], rc, ALU.mult)

            A(0)
            for u in range(1, len(units)):
                A(u)
                B(u - 1)
            B(len(units) - 1)

        def fox_prep(l):
            for ti, (nm, L) in enumerate((("fA", 96), ("fB", 64))):
                wv, _ = load_w(l, nm)
                rows = slice(0, L)
                cF = V(FP[0], FP[0].ap[rows, 0:2048], tuple(range(8)))
                s.ts(small[rows, 1:2], pc((l, "bfA" if ti == 0 else "bfB"), rows), -1.0, ALU.mult)
                for sp in range(4):
                    t0 = 512 * sp
                    ps = zb(sp)
                    proj_fm(ps[rows, :], wv, L, t0, 512)
                    e = V(FP[1], FP[1].ap[rows, 0:512], (0, 1))
                    s.act(e, ps[rows, :], AF.Exp, bias=small[rows, 1:2], scale=-1.0)
                    s.act(e, e, AF.Ln, bias=1.0)
                    init = 0.0 if sp == 0 else cF[:, t0 - 1:t0]
                    s.scan(cF[:, t0:t0 + 512], cc("ones", 512 // 4 if False else 128, rows)[:, 0:1].bc([L, 512]), e, init, ALU.mult, ALU.subtract)
                for b in range(16):
                    pt = zb(b % 4)[:, 0:L]
                    s.transpose(pt, cF[:, 128 * b:128 * b + 128], cc("ident", L, rows))
                    nh = 3 if ti == 0 else 2
                    s.copy(CT[:, b, 3 * ti:3 * ti + nh], pt[:, 0:32 * nh:32], eng="dve")
                hi = V(BP[5], BP[5].ap[rows, :], (0, 1, 2, 3))
                lo = V(BP[6], BP[6].ap[rows, :], (0, 1, 2, 3))
                r1 = V(FP[1], FP[1].ap[rows, 0:2048], tuple(range(8)))
                ng = NEGC[ti][rows, :]
                s.ts(hi, cF, -1.0, ALU.mult)
                s.stt(r1, cF, -1.0, hi, ALU.mult, ALU.subtract)
                s.copy(lo, r1, eng="dve")
                s.ts(ng, hi, cc("sel", 1, rows), ALU.mult)
                s.tt(r1, r1, lo, ALU.subtract)
                s.stt(ng, lo, cst[rows, CC["sel"] + 1:CC["sel"] + 2], ng, ALU.mult, ALU.add)
                s.copy(hi, r1, eng="dve")
                s.stt(ng, hi, cst[rows, CC["sel"] + 2:CC["sel"] + 3], ng, ALU.mult, ALU.add)
            s.memset(NCT.v(), 0.0)
            s.ts(NCT[:, :, 0:5], CT[:, :, 0:5], -1.0, ALU.mult)

        def out_proj(l, src, dst, nwkeys, last):
            o, L = GOFF["wo"]
            for t4 in range(4):
                s.dma(BP[t4].v(), V(wpack, wpack.ap[l, :, o + 2048 * t4:o + 2048 * t4 + 2048], (0,)), q="pool")
            if "a" in parts and not last:
                prefetch_layer(l + 1, False)
            wo = lambda k, j: BP[k // 2][:, 1024 * (k % 2) + 128 * j:1024 * (k % 2) + 128 * j + 128]
            srcv = src.ap.rearrange("(k p) t -> p k t", p=128)
            dstv = dst.ap.rearrange("(k p) t -> p k t", p=128)
            fin = []
            def issue_loads(sp_):
                for v_, k0, nk in xset(sp_):
                    s.dma(v_, V(src, srcv[:, k0:k0 + nk, 512 * sp_:512 * sp_ + 512], tuple(range(k0, k0 + nk))))
            issue_loads(0)
            for sp in range(4):
                t0 = 512 * sp
                pcs = xset(sp)
                if sp + 1 < 4:
                    issue_loads(sp + 1)
                sq = [V(BP[7 + i], BP[7 + i].ap.rearrange("p (k t) -> p k t", k=4), (0, 1, 2, 3)) for i in range(2)]
                for j in range(8):
                    ps = zb(j % 4)
                    for k in range(8):
                        s.mm(ps, wo(k, j), V(yT, yT.ap[:, k, t0:t0 + 512], (k,)), start=(k == 0), stop=(k == 7))
                    xj = xk(pcs, j)
                    s.tt(xj, ps, xj, ALU.add)
                    s.act(sq[j // 4][:, j % 4, :], xj, AF.Square)
                if not last:
                    for v_, k0, nk in pcs:
                        s.dma(V(dst, dstv[:, k0:k0 + nk, t0:t0 + 512], tuple(range(k0, k0 + nk))), v_)
                for k in range(8):
                    s.mm(P5.v(), cb("ones"), sq[k // 4][:, k % 4, :], start=(k == 0), stop=(k == 7))
                s.act(P6.v(), P5.v(), AF.Ln, bias=NEPS.v(), scale=1.0 / DM)
                s.act(P7.v(), P6.v(), AF.Exp, scale=-0.5)
                for k in range(8):
                    xk_ = xk(pcs, k)
                    if not last:
                        s.stt(hT[:, k, 1 + t0:1 + t0 + 512], xk_, pc(nwkeys(k)), P7.v(), ALU.mult, ALU.mult)
                    else:
                        s.stt(xk_, xk_, pc(nwkeys(k)), P7.v(), ALU.mult, ALU.mult)
                if last:
                    ov = outT.ap.rearrange("(k p) t -> p k t", p=128)
                    for v_, k0, nk in pcs:
                        fin.append(s.dma(V(outT, ov[:, k0:k0 + nk, t0:t0 + 512], tuple(range(k0, k0 + nk))), v_))
            return fin

        srcs = [xT, xs1, xs2]
        fin = []
        for l in range(nlayers):
            if l == 0:
                if "a" in parts:
                    prefetch_layer(0, True)
                norm_phase(srcs[l], lambda k, l=l: (l, "nw", k), True)
            gates_all(l)
            if "a" in parts:
                rwkv(l)
            else:
                for m in range(3):
                    s.memset(V(yT, yT.ap[:, m, :], (m,)), 0.0, eng="pool")
            if "b" in parts:
                for gi in range(3):
                    attn_group(l, "b", gi)
            if "c" in parts:
                fox_prep(l)
                for gi in range(3):
                    attn_group(l, "c", gi)
            if debug == l + 1:
                f = s.dma(dbg.v(), V(yT, yT.ap.rearrange("p k t -> p (k t)"), tuple(range(8))))
                print("deadlock check:", s.check())
                s.emit(final_waits=[f])
                return nc, s.stats()
            last = (l == nlayers - 1)
            fin = out_proj(l, srcs[l], srcs[l + 1], (lambda k: ("fnw", k)) if last else (lambda k, l=l: (l + 1, "nw", k)), last)
        s.emit(final_waits=fin)
        stats = s.stats()
    return nc, stats


_CACHE = {}


def kernel(**inputs):
    inp = {k: np.asarray(v) for k, v in inputs.items()}
    wpack, pcol, lora = _host_pack(inp)
    consts = _host_consts()
    if "nc" not in _CACHE:
        _CACHE["nc"] = build()[0]
    nc = _CACHE["nc"]
    x = inp["x"]
    in_maps = []
    for b in range(8):
        in_maps.append({"xT": np.ascontiguousarray(x[b].T), "wpack": wpack, "pcol": pcol,
                        "mu": np.ascontiguousarray(inp["mu"]), "lora": lora, "consts": consts})
    res = run_bass_kernel_spmd(nc, in_maps, core_ids=list(range(8)))
    out = np.stack([np.ascontiguousarray(r["outT"].T) for r in res.results]).astype(np.float32)
    return out
```

```python
import numpy as np
import concourse.bass as bass
import concourse.mybir as mybir
from concourse.bass_utils import run_bass_kernel_spmd
from contextlib import ExitStack

F32 = mybir.dt.float32
BF16 = mybir.dt.bfloat16
AF = mybir.ActivationFunctionType
ALU = mybir.AluOpType

ENGS = ("pe", "act", "dve", "pool", "sp")
NDSEM = 6


class V:
    __slots__ = ("t", "ap", "parts")

    def __init__(self, t, ap, parts):
        self.t, self.ap, self.parts = t, ap, parts

    def __getitem__(self, idx):
        return V(self.t, self.ap[idx], self.parts)

    def keys(self):
        return [(self.t.id, p) for p in self.parts]

    def v(self):
        return self


class Tile:
    _n = 0

    def __init__(self, ap, nparts=1, name=""):
        Tile._n += 1
        self.id = Tile._n
        self.ap = ap
        self.nparts = nparts
        self.name = name
        self.is_psum = False

    def __getitem__(self, idx):
        return V(self, self.ap[idx], tuple(range(self.nparts)))

    def v(self):
        return V(self, self.ap, tuple(range(self.nparts)))

    def part(self, *ps):
        return V(self, self.ap, tuple(ps))


class Op:
    __slots__ = ("eng", "fn", "waits", "sig", "sem", "val", "kind", "pre")


class Sched:
    def __init__(self, nc, stack):
        self.nc = nc
        self.stack = stack
        self.ops = {e: [] for e in ENGS}
        self.lastw = {}
        self.readers = {}
        self.sems = {e: stack.enter_context(nc.semaphore(f"s_{e}")) for e in ENGS}
        self.dsems = {q: [stack.enter_context(nc.semaphore(f"d_{q}{i}")) for i in range(NDSEM)]
                      for q in ("sp", "pool", "act")}
        self.dcount = {q: 0 for q in ("sp", "pool", "act")}
        self.dlast = {q: [None] * NDSEM for q in ("sp", "pool", "act")}
        self.same_engine_sync = True
        self.psum_ids = set()

    def sbuf(self, name, shape, dtype, nparts=1):
        t = self.stack.enter_context(self.nc.sbuf_tensor(name, list(shape), dtype))
        return Tile(t.ap(), nparts, name)

    def psum(self, name, shape, dtype=F32, nparts=1):
        t = self.stack.enter_context(self.nc.psum_tensor(name, list(shape), dtype))
        tl = Tile(t.ap(), nparts, name)
        tl.is_psum = True
        self.psum_ids.add(tl.id)
        return tl

    def dram(self, name, shape, dtype, kind, nparts=1):
        t = self.nc.dram_tensor(name, list(shape), dtype, kind=kind).ap()
        return Tile(t, nparts, name)

    def _add(self, eng, fn, reads, writes, kind="c", dmaq=None):
        op = Op()
        op.eng, op.fn, op.kind, op.sig, op.sem, op.val = eng, fn, kind, False, None, None
        op.pre = []
        deps = []
        rk = [k for v in reads for k in v.keys()]
        wk = [k for v in writes for k in v.keys()]
        for k in rk:
            w = self.lastw.get(k)
            if w is not None:
                deps.append(w)
            if k[0] in self.psum_ids:
                rd = self.readers.get(k)
                if rd:
                    for e2, o2 in rd[0].items():
                        if e2 != eng:
                            deps.append(o2)
        for k in wk:
            w = self.lastw.get(k)
            if w is not None:
                deps.append(w)
            rd = self.readers.get(k)
            if rd:
                deps.extend(rd[0].values())
                deps.extend(rd[1])
        fdeps = []
        seen_ids = set()
        for d in deps:
            if d is op:
                continue
            if d.kind == "c" and d.eng == eng and kind == "c":
                if eng == "pe" or not self.same_engine_sync:
                    continue
            if id(d) not in seen_ids:
                seen_ids.add(id(d))
                fdeps.append(d)
        op.waits = fdeps
        if kind == "d":
            q = dmaq
            n = self.dcount[q]
            self.dcount[q] += 1
            slot = n % NDSEM
            prev = self.dlast[q][slot]
            if prev is not None:
                op.waits.append(prev)
            self.dlast[q][slot] = op
            op.sem = self.dsems[q][slot]
            op.val = 16 * (n // NDSEM + 1)
            op.sig = True
        for d in fdeps:
            d.sig = True
        wks = set(wk)
        for k in wk:
            self.lastw[k] = op
            self.readers[k] = None
        for k in rk:
            if k not in wks:
                rd = self.readers.get(k)
                if rd is None:
                    rd = self.readers[k] = ({}, [])
                if kind == "d":
                    rd[1].append(op)
                else:
                    rd[0][eng] = op
        self.ops[eng].append(op)
        return op

    def mm(self, out, lhsT, rhs, start=True, stop=True, extra_reads=(), **kw):
        nc = self.nc
        return self._add("pe", lambda: nc.tensor.matmul(out.ap, lhsT.ap, rhs.ap, start=start, stop=stop, **kw),
                         [lhsT, rhs, *extra_reads], [out])

    def transpose(self, out, in_, ident):
        nc = self.nc
        return self._add("pe", lambda: nc.tensor.transpose(out.ap, in_.ap, ident.ap), [in_, ident], [out])

    def act(self, out, in_, func, bias=None, scale=None, accum_out=None, eng="act"):
        nc = self.nc
        reads = [in_]
        kw = {}
        if bias is not None:
            if isinstance(bias, V):
                reads.append(bias)
                kw["bias"] = bias.ap
            else:
                kw["bias"] = bias
        if scale is not None:
            if isinstance(scale, V):
                reads.append(scale)
                kw["scale"] = scale.ap
            else:
                kw["scale"] = scale
        writes = [out]
        if accum_out is not None:
            kw["accum_out"] = accum_out.ap
            writes.append(accum_out)
        return self._add("act", lambda: nc.scalar.activation(out.ap, in_.ap, func, **kw), reads, writes)

    def _e(self, eng):
        return {"dve": self.nc.vector, "pool": self.nc.gpsimd, "act": self.nc.scalar}[eng]

    def tt(self, out, in0, in1, op, eng="dve"):
        e = self._e(eng)
        return self._add(eng, lambda: e.tensor_tensor(out.ap, in0.ap, in1.ap, op), [in0, in1], [out])

    def ts(self, out, in0, s1, op0, s2=None, op1=None, eng="dve", accum_out=None):
        e = self._e(eng)
        reads = [in0]
        a1 = s1.ap if isinstance(s1, V) else s1
        a2 = s2.ap if isinstance(s2, V) else s2
        if isinstance(s1, V):
            reads.append(s1)
        if isinstance(s2, V):
            reads.append(s2)
        kw = {}
        writes = [out]
        if op1 is not None:
            kw["op1"] = op1
        if accum_out is not None:
            kw["accum_out"] = accum_out.ap
            writes.append(accum_out)
        return self._add(eng, lambda: e.tensor_scalar(out.ap, in0.ap, a1, a2, op0, **kw), reads, writes)

    def stt(self, out, in0, scalar, in1, op0, op1, eng="dve"):
        e = self._e(eng)
        reads = [in0, in1]
        a = scalar.ap if isinstance(scalar, V) else scalar
        if isinstance(scalar, V):
            reads.append(scalar)
        return self._add(eng, lambda: e.scalar_tensor_tensor(out.ap, in0.ap, a, in1.ap, op0, op1), reads, [out])

    def scan(self, out, d0, d1, initial, op0, op1):
        nc = self.nc
        reads = [d0, d1]
        a = initial.ap if isinstance(initial, V) else initial
        if isinstance(initial, V):
            reads.append(initial)
        return self._add("dve", lambda: nc.vector.tensor_tensor_scan(out.ap, d0.ap, d1.ap, a, op0, op1), reads, [out])

    def copy(self, out, in_, eng="dve"):
        if eng == "act":
            return self.act(out, in_, AF.Identity)
        e = self._e(eng)
        return self._add(eng, lambda: e.tensor_copy(out.ap, in_.ap), [in_], [out])

    def memset(self, out, val, eng="dve"):
        e = self._e(eng)
        return self._add(eng, lambda: e.memset(out.ap, val), [], [out])

    def recip(self, out, in_):
        nc = self.nc
        return self._add("dve", lambda: nc.vector.reciprocal(out.ap, in_.ap), [in_], [out])

    def dma(self, out, in_, q="sp", **kw):
        e = {"sp": self.nc.sync, "pool": self.nc.gpsimd, "act": self.nc.scalar}[q]
        return self._add(q, lambda: e.dma_start(out.ap, in_.ap, **kw), [in_], [out], kind="d", dmaq=q)

    def emit(self, final_waits=()):
        nc = self.nc
        for e in ENGS:
            c = 0
            for op in self.ops[e]:
                if op.kind == "c" and op.sig:
                    c += 1
                    op.sem = self.sems[e]
                    op.val = c
        engobj = {"pe": nc.tensor, "act": nc.scalar, "dve": nc.vector, "pool": nc.gpsimd, "sp": nc.sync}
        finals = list(final_waits)

        def run(e):
            eo = engobj[e]
            seen = {}
            nw = 0
            for op in self.ops[e]:
                for d in op.waits:
                    key = id(d.sem)
                    if seen.get(key, 0) >= d.val:
                        continue
                    seen[key] = d.val
                    eo.wait_ge(d.sem, d.val)
                    nw += 1
                inst = op.fn()
                if op.sig:
                    inst.then_inc(op.sem, 16 if op.kind == "d" else 1)
            if e == "sp":
                for d in finals:
                    eo.wait_ge(d.sem, d.val)
            return nw

        with nc.Block() as block:
            @block.tensor
            def _(x):
                run("pe")

            @block.scalar
            def _(x):
                run("act")

            @block.vector
            def _(x):
                run("dve")

            @block.gpsimd
            def _(x):
                run("pool")

            @block.sync
            def _(x):
                run("sp")

    def check(self):
        for e in ENGS:
            c = 0
            for op in self.ops[e]:
                if op.kind == "c" and op.sig:
                    c += 1
                    op.sem = self.sems[e]
                    op.val = c
        semv = {}
        ptr = {e: 0 for e in ENGS}
        progress = True
        while progress:
            progress = False
            for e in ENGS:
                while ptr[e] < len(self.ops[e]):
                    op = self.ops[e][ptr[e]]
                    if all(semv.get(id(d.sem), 0) >= d.val for d in op.waits):
                        if op.sig:
                            k = id(op.sem)
                            semv[k] = semv.get(k, 0) + (16 if op.kind == "d" else 1)
                            if op.kind == "c":
                                assert semv[k] == op.val, (e, ptr[e], semv[k], op.val)
                        ptr[e] += 1
                        progress = True
                    else:
                        break
        stuck = {e: (ptr[e], len(self.ops[e])) for e in ENGS if ptr[e] < len(self.ops[e])}
        return stuck

    def stats(self):
        return {e: len(self.ops[e]) for e in ENGS}


def _v_bc(self, shape):
    return V(self.t, self.ap.broadcast_to(list(shape)), self.parts)


V.bc = _v_bc


S = 2048
DM = 1024
OFF_QB, OFF_KB, OFF_VB = 1280, 1600, 1920
OFF_QC, OFF_KC, OFF_VC = 2240, 2560, 2880
OFF_F, OFF_G = 3200, 3205
EPS = 1e-6
GN_EPS = 64e-5


def _groups():
    g = []
    g.append(("lora", list(range(1152, 1280))))
    for p in range(3):
        g.append((f"r{p}", list(range(128 * p, 128 * p + 128))))
        g.append((f"k{p}", list(range(384 + 128 * p, 384 + 128 * p + 128))))
        g.append((f"v{p}", list(range(768 + 128 * p, 768 + 128 * p + 128))))
    for nm, oq, ok_, ov in (("b", OFF_QB, OFF_KB, OFF_VB),):
        g.append(("qb0", list(range(oq, oq + 128)))); g.append(("qb1", list(range(oq + 128, oq + 256)))); g.append(("qb2", list(range(oq + 256, oq + 320))))
        g.append(("kb0", list(range(ok_, ok_ + 128)))); g.append(("kb1", list(range(ok_ + 128, ok_ + 256)))); g.append(("kb2", list(range(ok_ + 256, ok_ + 320))))
        g.append(("vb0", list(range(ov, ov + 128)))); g.append(("vb1", list(range(ov + 128, ov + 256)))); g.append(("vb2", list(range(ov + 256, ov + 320))))
    oq, ok_, ov = OFF_QC, OFF_KC, OFF_VC
    g.append(("qc0", list(range(oq, oq + 64)))); g.append(("qc1", list(range(oq + 64, oq + 192)))); g.append(("qc2", list(range(oq + 192, oq + 320))))
    g.append(("kc0", list(range(ok_, ok_ + 64)))); g.append(("kc1", list(range(ok_ + 64, ok_ + 192)))); g.append(("kc2", list(range(ok_ + 192, ok_ + 320))))
    g.append(("vc0", list(range(ov, ov + 64)))); g.append(("vc1", list(range(ov + 64, ov + 192)))); g.append(("vc2", list(range(ov + 192, ov + 320))))
    g.append(("fA", [OFF_F + j // 32 for j in range(96)]))
    g.append(("fB", [OFF_F + 3 + j // 32 for j in range(64)]))
    for m in range(8):
        g.append((f"g{m}", list(range(OFF_G + 128 * m, OFF_G + 128 * m + 128))))
    return g


GROUPS = _groups()
GOFF = {}
_o = 0
for _n, _c in GROUPS:
    GOFF[_n] = (_o, len(_c))
    _o += 8 * len(_c)
GOFF["wo"] = (_o, 1024)
_o += 8 * 1024
WTOT = _o

PC = {}
_pc = 0
for _l in range(2):
    for _nm in ("w0", "a0", "k_k", "k_a", "r_k", "ln_w", "ln_b"):
        for _p in range(3):
            PC[(_l, _nm, _p)] = _pc; _pc += 1
    PC[(_l, "bfA")] = _pc; _pc += 1
    PC[(_l, "bfB")] = _pc; _pc += 1
    for _k in range(8):
        PC[(_l, "nw", _k)] = _pc; _pc += 1
for _k in range(8):
    PC[("fnw", _k)] = _pc; _pc += 1
NPC = _pc

CC = {"ident": 0, "strict_ts": 128, "incl_ts": 256, "strict_st": 384, "incl_st": 512, "blk": 640, "ones": 768, "sel": 896,
      "nm_strict_ts": 899, "nm_incl_st": 1027, "nones": 1155, "zero": 1283}
NCONST = 1284
NEGBIG = -30000.0


def _host_consts():
    c = np.zeros((128, NCONST), np.float32)
    i = np.arange(128)
    c[:, 0:128] = np.eye(128)
    c[:, 128:256] = (i[None, :] < i[:, None])
    c[:, 256:384] = (i[None, :] <= i[:, None])
    c[:, 384:512] = (i[:, None] < i[None, :])
    c[:, 512:640] = (i[:, None] <= i[None, :])
    c[:, 640:768] = (i[:, None] // 64 == i[None, :] // 64)
    c[:, 768:896] = 1.0
    for r in range(3):
        c[:, 896 + r] = (i % 32 == r)
    c[:, 899:1027] = np.where(i[None, :] < i[:, None], 0.0, NEGBIG)
    c[:, 1027:1155] = np.where(i[:, None] <= i[None, :], 0.0, NEGBIG)
    c[:, 1155:1283] = -1.0
    return c


def _host_pack(inp):
    wpack = np.empty((2, 128, WTOT), np.float32)
    for l in range(2):
        W = inp["w_in"][l]
        for n, cols in GROUPS:
            o, L = GOFF[n]
            a = W[:, cols].reshape(8, 128, L).transpose(1, 0, 2).reshape(128, 8 * L)
            wpack[l, :, o:o + 8 * L] = a
        Wo = inp["w_out"][l]
        o, L = GOFF["wo"]
        wpack[l, :, o:o + 8 * L] = Wo.reshape(8, 128, 1024).transpose(1, 0, 2).reshape(128, 8 * 1024)
    pcol = np.zeros((128, NPC), np.float32)
    src = {"w0": "w0", "a0": "a0", "k_k": "k_k", "k_a": "k_a", "ln_w": "ln_x_w", "ln_b": "ln_x_b"}
    for l in range(2):
        for nm in ("w0", "a0", "k_k", "k_a", "r_k", "ln_w", "ln_b"):
            v = inp["r_k"][l].reshape(384) if nm == "r_k" else inp[src[nm]][l]
            for p in range(3):
                pcol[:, PC[(l, nm, p)]] = v[128 * p:128 * p + 128]
        bf = inp["b_f"][l]
        pcol[0:96, PC[(l, "bfA")]] = bf[np.arange(96) // 32]
        pcol[0:64, PC[(l, "bfB")]] = bf[3 + np.arange(64) // 32]
        for k in range(8):
            pcol[:, PC[(l, "nw", k)]] = inp["norm_w"][l][128 * k:128 * k + 128]
    for k in range(8):
        pcol[:, PC[("fnw", k)]] = inp["final_norm_w"][128 * k:128 * k + 128]
    lora = np.stack([np.stack([inp["w_up"][l], inp["a_up"][l]]) for l in range(2)]).astype(np.float32)
    return wpack, pcol, lora


def build(debug=None, nlayers=2, parts=("a", "b", "c")):
    nc = bass.Bass("TRN2", target_bir_lowering=False)
    st = ExitStack()
    with st:
        s = Sched(nc, st)
        xT = s.dram("xT", [DM, S], F32, "ExternalInput", nparts=8)
        wpack = s.dram("wpack", [2, 128, WTOT], F32, "ExternalInput")
        pcol_d = s.dram("pcol", [128, NPC], F32, "ExternalInput")
        mu_d = s.dram("mu", [2, 1280], F32, "ExternalInput")
        lora_d = s.dram("lora", [2, 2, 64, 384], F32, "ExternalInput")
        const_d = s.dram("consts", [128, NCONST], F32, "ExternalInput")
        xs1 = s.dram("xs1", [DM, S], F32, "Internal", nparts=8)
        xs2 = s.dram("xs2", [DM, S], F32, "Internal", nparts=8)
        outT = s.dram("outT", [DM, S], F32, "ExternalOutput", nparts=8)
        dbg = s.dram("dbg", [128, 8 * S], BF16, "ExternalOutput") if debug else None

        hT = s.sbuf("hT", [128, 8, S + 1], BF16)
        yT = s.sbuf("yT", [128, 8, S], BF16, nparts=8)
        WR = [s.sbuf(f"wr{i}", [128, 8 * 320], BF16) for i in range(2)]
        stage2 = s.sbuf("stage2", [128, 8, 128], F32)
        murep = s.sbuf("murep", [128, 1280], F32)
        cst = s.sbuf("cst", [128, NCONST], F32)
        cbf = s.sbuf("cbf", [128, NCONST], BF16)
        pcol = s.sbuf("pcolsb", [128, NPC], F32)
        pder = s.sbuf("pder", [128, 16], F32)
        lorabf = s.sbuf("lorabf", [64, 2, 384], BF16)
        FP = [s.sbuf(f"fp{i}", [128, 2052], F32, nparts=8) for i in range(4)]

        def _shv(t, c0, parts):
            return V(t, t.ap[:, c0:c0 + 1024].bitcast(BF16).rearrange("p (a k n) -> p a k n", a=2, k=8), parts)
        SH = [_shv(FP[2], 0, (0, 1, 2, 3)), _shv(FP[2], 1024, (4, 5, 6, 7)), _shv(FP[3], 0, (0, 1, 2, 3))]
        stage = V(FP[3], FP[3].ap[:, 1024:2048].rearrange("p (k n) -> p k n", k=8), (4, 5, 6, 7))
        BP = [s.sbuf(f"bp{i}", [128, 2048], BF16, nparts=4) for i in range(9)]
        ART = s.sbuf("art", [128, 16, 256], BF16)
        ART_flat = V(ART, ART.ap.rearrange("p c n -> p (c n)")[:, 0:2048], (0,))
        GC = s.sbuf("gc", [128, 16], F32)
        Hf = s.sbuf("Hf", [128, 64], F32)
        Hbz = s.sbuf("Hbz", [128, 2, 64], BF16)
        small = s.sbuf("small", [128, 64], F32)
        NT = [s.sbuf(f"negtot{i}", [128, 1], F32) for i in range(2)]
        EPS24 = s.sbuf("eps24", [128, 1], F32)
        GNEPS = s.sbuf("gneps", [128, 1], F32)
        NEPS = s.sbuf("neps", [128, 1], F32)
        CR = [s.sbuf(f"carry{i}", [128, 1], F32) for i in range(2)]
        NCT = s.sbuf("nctm", [128, 16, 8], F32)
        CT = s.sbuf("ctm", [128, 16, 8], F32)
        CSETS = [dict(Pm=[s.sbuf(f"Pm{j}{i}", [128, 2, 128], BF16) for i in range(2)],
                      QTm=[s.sbuf(f"QTm{j}{i}", [128, 2, 256], BF16) for i in range(2)],
                      AKK=s.sbuf(f"akk{j}", [128, 2, 256], BF16),
                      ARB=s.sbuf(f"arb{j}", [128, 2, 128], BF16))
                 for j in range(6)]
        Xb = s.sbuf("Xb", [128, 2, 64], BF16)
        Ub = s.sbuf("Ub", [128, 2, 64], BF16)

        PS = s.psum("PS", [128, 4096], F32, nparts=8)

        def bank(b):
            return V(PS, PS.ap[:, 512 * b:512 * b + 512], (b,))

        TP = V(PS, PS.ap[:, 3072:3584].bitcast(BF16), (6,))
        P5, P6, P7 = bank(7), bank(4), bank(5)
        NEGC = [BP[7], BP[8]]

        def zb(b):
            return bank(b)

        def zr(W, base=0):
            return V(PS, PS.ap[:, 512 * base:512 * base + W], tuple(range(base, base + (W + 511) // 512)))

        def fq(i, q, n=1, w=256):
            return V(FP[i], FP[i].ap[:, w * q:w * (q + n)], tuple(range(q * w // 256, (q + n) * w // 256)))

        def bq(i, q, w=512):
            return V(BP[i], BP[i].ap[:, w * q:w * q + w], (q,))

        def cc(name, n=128, rows=slice(0, 128)):
            return cst[rows, CC[name]:CC[name] + n]

        def cb(name, n=128, rows=slice(0, 128)):
            return cbf[rows, CC[name]:CC[name] + n]

        def pc(key, rows=slice(0, 128)):
            return pcol[rows, PC[key]:PC[key] + 1]

        s.dma(cst.v(), const_d.v())
        s.dma(cbf.v(), const_d.v(), q="pool")
        s.dma(pcol.v(), pcol_d.v())
        s.memset(hT[:, :, 0:1], 0.0, eng="pool")
        s.memset(EPS24.v(), 1e-24)
        s.memset(GNEPS.v(), GN_EPS)
        s.memset(NEPS.v(), EPS)
        if debug:
            s.memset(yT.v(), 0.0, eng="pool")

        wr_i = [0]

        def load_w(l, name):
            o, L = GOFF[name]
            t = WR[wr_i[0] % 2]
            wr_i[0] += 1
            s.dma(t[:, 0:8 * L], wpack[l, :, o:o + 8 * L], q="pool")
            return V(t, t.ap[:, 0:8 * L].rearrange("p (k n) -> p k n", k=8), (0,)), L

        sh_i = [0]

        def load_shift(l, name, mu0):
            o, L = GOFF[name]
            t = SH[sh_i[0] % 3]
            sh_i[0] += 1
            s.dma(stage, V(wpack, wpack.ap[l, :, o:o + 8 * L].rearrange("p (k n) -> p k n", k=8), (0,)))
            mub = V(murep, murep.ap[:, mu0:mu0 + 128].rearrange("p (o n) -> p o n", o=1).broadcast_to([128, 8, 128]), (0,))
            s.tt(stage2.v(), stage, mub, ALU.mult, eng="pool")
            s.copy(t[:, 1], stage2.v(), eng="pool")
            s.tt(t[:, 0], stage, stage2.v(), ALU.subtract, eng="pool")
            return t

        WSH = {}

        def get_shift(l, name, mu0):
            if (l, name) not in WSH:
                WSH[(l, name)] = load_shift(l, name, mu0)
            return WSH[(l, name)]

        def prefetch_layer(l, full):
            s.dma(murep.v(), V(mu_d, mu_d.ap[l].partition_broadcast(128), (0,)))
            s.dma(lorabf.v(), V(lora_d, lora_d.ap[l].rearrange("a j c -> j a c"), (0,)), q="pool")
            get_shift(l, "lora", 1152)
            get_shift(l, "r0", 0)
            get_shift(l, "k0", 384)

        def proj_fm(out, wv, L, t0, n, wv2=None):
            last = 7
            for k in range(8):
                s.mm(out, wv[:, k, 0:L], hT[:, k, 1 + t0:1 + t0 + n], start=(k == 0), stop=(k == last and wv2 is None))
            if wv2 is not None:
                for k in range(8):
                    s.mm(out, wv2[:, k, 0:L], hT[:, k, t0:t0 + n], start=False, stop=(k == last))

        def xset(sp):
            if sp % 2 == 0:
                return [(V(FP[i], FP[i].ap[:, 0:2048].rearrange("p (k t) -> p k t", k=4), tuple(range(8))), 4 * i, 4) for i in range(2)]
            artf = V(ART, ART.ap.rearrange("p c n -> p (c n)").bitcast(F32).rearrange("p (k t) -> p k t", k=4), (0,))
            b4 = V(BP[4], BP[4].ap.bitcast(F32).rearrange("p (k t) -> p k t", k=2), (0, 1, 2, 3))
            b5 = V(BP[5], BP[5].ap.bitcast(F32).rearrange("p (k t) -> p k t", k=2), (0, 1, 2, 3))
            return [(artf, 0, 4), (b4, 4, 2), (b5, 6, 2)]

        def xk(pieces, k):
            for v_, k0, nk in pieces:
                if k0 <= k < k0 + nk:
                    return v_[:, k - k0, :]

        def norm_phase(src, nwkeys, to_h, dst=None):
            srcv = src.ap.rearrange("(k p) t -> p k t", p=128)
            fin = []

            def issue(sp_):
                for v_, k0, nk in xset(sp_):
                    s.dma(v_, V(src, srcv[:, k0:k0 + nk, 512 * sp_:512 * sp_ + 512], tuple(range(k0, k0 + nk))), q="act")
            issue(0)
            issue(1)
            for sp in range(4):
                t0 = 512 * sp
                pcs = xset(sp)
                sq = [V(BP[7 + i], BP[7 + i].ap.rearrange("p (k t) -> p k t", k=4), (0, 1, 2, 3)) for i in range(2)]
                for v_, k0, nk in pcs:
                    s.act(sq[k0 // 4][:, k0 % 4:k0 % 4 + nk, :], v_, AF.Square)
                if 1 <= sp < 3:
                    issue(sp + 1)
                for k in range(8):
                    s.mm(P5.v(), cb("ones"), sq[k // 4][:, k % 4, :], start=(k == 0), stop=(k == 7))
                s.act(P6.v(), P5.v(), AF.Ln, bias=NEPS.v(), scale=1.0 / DM)
                s.act(P7.v(), P6.v(), AF.Exp, scale=-0.5)
                for k in range(8):
                    s.stt(hT[:, k, 1 + t0:1 + t0 + 512], xk(pcs, k), pc(nwkeys(k)), P7.v(), ALU.mult, ALU.mult)
            return fin

        def gates_all(l):
            for m in range(8):
                wv, L = load_w(l, f"g{m}")
                for sp in range(4):
                    t0 = 512 * sp
                    ps = zb((4 * m + sp) % 4)
                    proj_fm(ps, wv, 128, t0, 512)
                    s.act(V(yT, yT.ap[:, m, t0:t0 + 512], (m,)), ps, AF.Silu)

        def gate_and_store(l, m):
            wv, L = load_w(l, f"g{m}")
            for sp in range(4):
                t0 = 512 * sp
                ps = zb(sp % 4)
                proj_fm(ps, wv, 128, t0, 512)
                g = fq(0, 2 * (sp % 2), 2)
                s.act(g, ps, AF.Silu)
                yv = V(yT, yT.ap[:, m, t0:t0 + 512], (m,))
                s.tt(yv, yv, g, ALU.mult)

        def rwkv(l):
            TW, AL, BT, KT, BTM, KTM, VTM, BS, BON = BP
            for p in range(3):
                s.ts(pder[:, p:p + 1], pc((l, "w0", p)), -1.0, ALU.mult)
                s.ts(pder[:, 3 + p:4 + p], pc((l, "k_a", p)), -1.0, ALU.mult, 1.0, ALU.add)
                s.ts(pder[:, 6 + p:7 + p], pc((l, "a0", p)), -1.0, ALU.mult)
            shl = get_shift(l, "lora", 1152)
            for sp in range(4):
                t0 = 512 * sp
                ps = zb(sp % 4)
                proj_fm(ps, shl[:, 0], 128, t0, 512, shl[:, 1])
                s.act(TW[0:64, t0:t0 + 512], ps[0:64, :], AF.Tanh)
                s.copy(AL[0:64, t0:t0 + 512], ps[64:128, :], eng="dve")
            import os
            RW = int(os.environ.get("RW_STOP", "99"))
            if RW <= 1:
                return
            for p in range(3 if RW > 4 else 1):
                shr = get_shift(l, f"r{p}", 128 * p)
                shk = get_shift(l, f"k{p}", 384 + 128 * p)
                shv = get_shift(l, f"v{p}", 768 + 128 * p)
                negw0 = pder[:, p:p + 1]
                omka = pder[:, 3 + p:4 + p]
                N = 256
                nega0 = pder[:, 6 + p:7 + p]
                t_r, t_k, t_v, t_e, t_c, t_a = [fq(0, i) for i in range(6)]
                t_kk, t_sq = fq(0, 6), fq(0, 7)
                t_ri, t_kp, t_b, t_x = fq(1, 0), fq(1, 1), fq(1, 2), fq(1, 3)
                r3 = lambda v: V(v.t, v.ap.rearrange("p (c t) -> p c t", t=128), v.parts)
                nch = N // 128

                def part1(sp, which):
                    t0 = N * sp
                    cs = slice(t0, t0 + N)
                    if which == 0:
                        proj_fm(zb(0)[:, 0:N], shr[:, 0], 128, t0, N, shr[:, 1])
                    elif which == 1:
                        proj_fm(zb(1)[:, 0:N], shk[:, 0], 128, t0, N, shk[:, 1])
                    else:
                        proj_fm(zb(2)[:, 0:N], shv[:, 0], 128, t0, N, shv[:, 1])
                        s.mm(zb(3)[:, 0:N], lorabf[:, 0, 128 * p:128 * p + 128], TW[0:64, cs])
                        s.mm(P5[:, 0:N], lorabf[:, 1, 128 * p:128 * p + 128], AL[0:64, cs])

                bfv = lambda v: V(v.t, v.ap.bitcast(BF16), v.parts)
                sq_rk = bfv(fq(1, 4))
                sqb, rkb = sq_rk[:, 0:N], sq_rk[:, N:2 * N]
                tbuf = lambda sp, q: bq(7, q)[:, N * (sp % 2):N * (sp % 2) + N]

                def part2a(sp):
                    pr, pk, pv, pw = zb(0)[:, 0:N], zb(1)[:, 0:N], zb(2)[:, 0:N], zb(3)[:, 0:N]
                    pa = P5[:, 0:N]
                    s.copy(t_r, pr, eng="dve")
                    s.act(t_k, pk, AF.Identity)
                    s.copy(t_v, pv, eng="dve")
                    s.act(tbuf(sp, 2), pv, AF.Identity)
                    s.act(t_e, pw, AF.Exp, bias=negw0, scale=-1.0)
                    s.act(t_a, pa, AF.Exp, bias=nega0, scale=-1.0)

                def part2b(sp, stage):
                    t0 = N * sp
                    cs = slice(t0, t0 + N)
                    pss, pbn = P6[:, 0:N], P7[:, 0:N]
                    if stage == 1:
                        s.act(t_ri, pss, AF.Ln, bias=EPS24.v())
                        s.act(t_ri, t_ri, AF.Exp, scale=-0.5)
                        s.tt(t_kk, t_kk, t_ri, ALU.mult)
                        s.tt(t_b, t_kk, t_a, ALU.mult, eng="pool")
                        s.ts(t_x, t_a, pc((l, "k_a", p)), ALU.mult, omka, ALU.add)
                        s.tt(t_kp, t_k, t_x, ALU.mult, eng="pool")
                        s.stt(rkb, t_r, pc((l, "r_k", p)), t_kp, ALU.mult, ALU.mult)
                        s.mm(pbn, cb("blk"), rkb)
                        return
                    if stage == 2:
                        s.tt(BON[:, cs], pbn, t_v, ALU.mult)
                        return
                    s.act(t_e, t_e, AF.Ln, bias=1.0)
                    s.act(t_e, t_e, AF.Exp, bias=-0.5, scale=-1.0)
                    s.act(t_a, t_a, AF.Ln, bias=1.0)
                    s.act(t_a, t_a, AF.Exp, scale=-1.0)
                    for ch in range(nch):
                        cc_ = slice(128 * ch, 128 * ch + 128)
                        s.scan(t_c[:, cc_], cc("ones"), t_e[:, cc_], 0.0, ALU.mult, ALU.subtract)
                    s.ts(t_kk, t_k, pc((l, "k_k", p)), ALU.mult)
                    s.tt(sqb, t_kk, t_kk, ALU.mult)
                    s.mm(pss, cb("blk"), sqb)

                def part3a(sp):
                    t0 = N * sp
                    cs = slice(t0, t0 + N)
                    s.tt(t_k, t_c, t_e, ALU.add, eng="pool")
                    s.act(t_k, t_k, AF.Exp)
                    s.act(t_a, t_c, AF.Exp, scale=-1.0)
                    s.act(t_sq, t_c, AF.Exp)
                    for ch in range(nch):
                        cc_ = slice(128 * ch, 128 * ch + 128)
                        s.act(t_ri[:, cc_], t_c[:, cc_], AF.Exp, bias=t_c[:, 128 * ch + 127:128 * ch + 128], scale=-1.0)
                    c0 = t0 // 128
                    s.copy(GC[:, c0:c0 + nch], t_sq[:, 127:N:128], eng="dve")
                    s.stt(ART[:, c0:c0 + nch, 0:128], r3(t_kk), -1.0, r3(t_k), ALU.mult, ALU.mult)
                    s.tt(ART[:, c0:c0 + nch, 128:256], r3(t_r), r3(t_sq), ALU.mult, eng="pool")
                    s.tt(BT[:, cs], t_b, t_a, ALU.mult, eng="pool")
                    s.tt(KT[:, cs], t_kp, t_a, ALU.mult, eng="pool")
                    s.tt(tbuf(sp, 0), t_b, t_ri, ALU.mult)
                    s.tt(tbuf(sp, 1), t_kp, t_ri, ALU.mult)

                def part3t(sp):
                    for j in range(3):
                        src_ = tbuf(sp, j)
                        for ch in range(nch):
                            s.transpose(TP[:, 128 * (2 * j + ch):128 * (2 * j + ch) + 128], src_[:, 128 * ch:128 * ch + 128], cb("ident"))

                def part3e(sp):
                    cs = slice(N * sp, N * sp + N)
                    for j, dstt in enumerate((BTM, KTM, VTM)):
                        s.copy(dstt[:, cs], TP[:, 256 * j:256 * j + N], eng=("dve" if j == 1 else "act"))

                nsp = 8
                for w_ in range(3):
                    part1(0, w_)
                for sp in range(nsp):
                    nx = sp + 1 < nsp
                    part2a(sp)
                    if nx:
                        part1(sp + 1, 0)
                    if sp > 0:
                        part3e(sp - 1)
                    part2b(sp, 0)
                    if nx:
                        part1(sp + 1, 1)
                    part2b(sp, 1)
                    if nx:
                        part1(sp + 1, 2)
                    part2b(sp, 2)
                    part3a(sp)
                    part3t(sp)
                part3e(nsp - 1)
                if p < 2:
                    get_shift(l, f"r{p + 1}", 128 * (p + 1))
                    get_shift(l, f"k{p + 1}", 384 + 128 * (p + 1))
                    get_shift(l, f"v{p + 1}", 768 + 128 * (p + 1))
                if RW <= 2:
                    continue
                s.memset(Hf.v(), 0.0)
                s.memset(Hbz.v(), 0.0)
                hs = [slice(0, 64), slice(64, 128)]
                bc3 = lambda name, n: V(cst, cst.ap[:, CC[name]:CC[name] + n], (0,))
                r2 = lambda v, n: V(v.t, v.ap.rearrange("p (e n) -> p e n", e=2), v.parts)
                idb = V(cbf, cbf.ap[:, 0:128].rearrange("p (o n) -> p o n", o=1).broadcast_to([128, 2, 128]), (0,))

                def chain(c, CS, bA, bB):
                    Pm, QTm, AKK, ARB, fin = CS["Pm"], CS["QTm"], CS["AKK"], CS["ARB"], CS
                    ccs = slice(128 * c, 128 * c + 128)
                    for e in range(2):
                        pP, pQ, pK = bank(bA)[:, 0:128], bank(bA)[:, 128:384], bank(bB)[:, 0:256]
                        s.mm(pP, ART[hs[e], c, 0:128], BT[hs[e], ccs]); yield
                        s.mm(pQ, BT[hs[e], ccs], ART[hs[e], c, :]); yield
                        s.mm(pK, KT[hs[e], ccs], ART[hs[e], c, :]); yield
                        s.tt(Pm[0][:, e, :], pP, bc3("strict_ts", 128), ALU.mult); yield
                        s.tt(QTm[0][:, e, 0:128], pQ[:, 0:128], bc3("strict_st", 128), ALU.mult); yield
                        s.tt(ARB[:, e, :], pQ[:, 128:256], bc3("incl_st", 128), ALU.mult); yield
                        s.tt(AKK[:, e, :], pK, bc3("strict_st", 256), ALU.mult); yield
                    s.act(QTm[0][:, :, 128:256], idb, AF.Identity); yield
                    cur = 0
                    for lev in range(6):
                        nxt = 1 - cur
                        pA, pB = bank(bA), bank(bB)[:, 0:256]
                        for e in range(2):
                            s.mm(pA[:, 256 * e:256 * e + 256], Pm[cur][:, e, :], QTm[cur][:, e, :]); yield
                            s.mm(pB[:, 128 * e:128 * e + 128], QTm[cur][:, e, 0:128], Pm[cur][:, e, :]); yield
                        pA3 = r2(pA, 256)
                        s.act(QTm[nxt][:, :, 0:128], pA3[:, :, 0:128], AF.Identity); yield
                        s.tt(QTm[nxt][:, :, 128:256], pA3[:, :, 128:256], QTm[cur][:, :, 128:256], ALU.add); yield
                        s.act(Pm[nxt].v(), r2(pB, 128), AF.Identity); yield
                        cur = nxt
                    pA = bank(bA)[:, 0:256]
                    for e in range(2):
                        s.mm(pA[:, 128 * e:128 * e + 128], Pm[cur][:, e, :], QTm[cur][:, e, 128:256]); yield
                    s.tt(QTm[1 - cur][:, :, 128:256], r2(pA, 128), QTm[cur][:, :, 128:256], ALU.add); yield
                    fin["TT"] = QTm[1 - cur]

                def serial(c, CS):
                    AKK, ARB, TTf = CS["AKK"], CS["ARB"], CS["TT"]
                    pX, pU = P5[:, 0:128], P5[:, 128:256]
                    for e in range(2):
                        s.mm(pX[:, 64 * e:64 * e + 64], ART[:, c, 0:128], Hbz[:, e, :], start=True, stop=False); yield
                        s.mm(pX[:, 64 * e:64 * e + 64], AKK[:, e, 0:128], VTM[:, 128 * c + 64 * e:128 * c + 64 * e + 64], start=False, stop=True); yield
                    s.copy(Xb.v(), r2(pX, 64), eng="dve"); yield
                    for e in range(2):
                        s.mm(pU[:, 64 * e:64 * e + 64], TTf[:, e, 128:256], Xb[:, e, :]); yield
                    s.act(Ub.v(), r2(pU, 64), AF.Identity); yield
                    pH = P5[:, 256:320]
                    for e in range(2):
                        s.mm(pH[hs[e], :], BTM[:, 128 * c + 64 * e:128 * c + 64 * e + 64], Ub[:, e, :], start=True, stop=False); yield
                        s.mm(pH[hs[e], :], KTM[:, 128 * c + 64 * e:128 * c + 64 * e + 64], VTM[:, 128 * c + 64 * e:128 * c + 64 * e + 64], start=False, stop=True); yield
                    pY = P7[:, 128 * (c % 4):128 * (c % 4) + 128]
                    for e in range(2):
                        s.mm(pY[hs[e], :], Hbz[:, e, :], ART[:, c, 128:256], start=True, stop=False); yield
                        s.mm(pY[hs[e], :], Ub[:, e, :], ARB[:, e, :], start=False, stop=False); yield
                        s.mm(pY[hs[e], :], VTM[:, 128 * c + 64 * e:128 * c + 64 * e + 64], AKK[:, e, 128:256], start=False, stop=True); yield
                    s.stt(Hf.v(), Hf.v(), GC[:, c:c + 1], pH, ALU.mult, ALU.add); yield
                    s.tt(Hbz.v(), V(Hf, Hf.ap.rearrange("p (o n) -> p o n", o=1).broadcast_to([128, 2, 64]), (0,)),
                         V(cst, cst.ap[:, CC["blk"]:CC["blk"] + 128].rearrange("p (e n) -> p e n", e=2), (0,)), ALU.mult); yield
                    if c % 4 == 3:
                        t0 = 512 * (c // 4)
                        cs5 = slice(t0, t0 + 512)
                        y = fq(1, 4, 2)
                        s.act(y, P7.v(), AF.Identity); yield
                        yb_, ysqb_ = bq(7, 0), bq(7, 1)
                        s.act(yb_, P7.v(), AF.Identity); yield
                        s.act(ysqb_, P7.v(), AF.Square); yield
                        pm_, pe_ = P5.v(), P5.v()
                        s.mm(pm_, cb("blk"), yb_); yield
                        mean = fq(0, 0, 2)
                        s.act(mean, pm_, AF.Identity, scale=1.0 / 64); yield
                        s.mm(pe_, cb("blk"), ysqb_); yield
                        var = fq(0, 2, 2)
                        s.tt(var, mean, mean, ALU.mult, eng="pool"); yield
                        s.stt(var, pe_, 1.0 / 64, var, ALU.mult, ALU.subtract); yield
                        s.act(var, var, AF.Ln, bias=GNEPS.v()); yield
                        s.act(var, var, AF.Exp, scale=-0.5); yield
                        s.tt(y, y, mean, ALU.subtract, eng="pool"); yield
                        s.tt(y, y, var, ALU.mult, eng="pool"); yield
                        s.ts(y, y, pc((l, "ln_w", p)), ALU.mult, pc((l, "ln_b", p)), ALU.add); yield
                        yv = V(yT, yT.ap[:, p, cs5], (p,))
                        s.tt(y, y, BON[:, cs5], ALU.add, eng="pool"); yield
                        s.tt(yv, y, yv, ALU.mult); yield

                def interleave(ga, gb, ka=3):
                    da = db = False
                    while not (da and db):
                        for _ in range(ka):
                            if not da:
                                try:
                                    next(ga)
                                except StopIteration:
                                    da = True
                        if not db:
                            try:
                                next(gb)
                            except StopIteration:
                                db = True

                def lockstep(gens):
                    gens = list(gens)
                    while gens:
                        for g_ in list(gens):
                            try:
                                next(g_)
                            except StopIteration:
                                gens.remove(g_)
                        yield

                def seq(gens):
                    for g_ in gens:
                        yield from g_

                CB = [(0, 1), (2, 3), (4, 6)]
                nchunks = 16 if RW > 3 else 1
                groups = [list(range(g0, min(g0 + 3, nchunks))) for g0 in range(0, nchunks, 3)]
                mk = lambda grp: lockstep([chain(c, CSETS[c % 6], *CB[c % 3]) for c in grp])
                for _ in mk(groups[0]):
                    pass
                for gi_, grp in enumerate(groups):
                    gb_ = seq([serial(c, CSETS[c % 6]) for c in grp])
                    if gi_ + 1 < len(groups):
                        interleave(mk(groups[gi_ + 1]), gb_, ka=1)
                    else:
                        for _ in gb_:
                            pass

        def attn_group(l, kind, gi):
            QT, KTt, A, AT, VT = BP[0], BP[1], BP[2], BP[3], BP[4]
            E_, F_ = FP[0], FP[1]
            if kind == "b":
                s.memset(E_[:, 0:1], 0.0)
            wq, Lq = load_w(l, f"q{kind}{gi}")
            if kind == "b":
                heads = [(2 * gi + e, 64 * e) for e in range(2 if gi < 2 else 1)]
            else:
                heads = [(0, 64)] if gi == 0 else [(2 * gi - 1 + e, 64 * e) for e in range(2)]
            rows = slice(64, 128) if (kind == "c" and gi == 0) else slice(0, Lq)
            for sp in range(4):
                ps = zb(sp)
                proj_fm(ps[rows, :], wq, Lq, 512 * sp, 512)
                s.act(QT[rows, 512 * sp:512 * sp + 512], ps[rows, :], AF.Identity, scale=0.125)
            wk, Lk = load_w(l, f"k{kind}{gi}")
            for sp in range(4):
                ps = zb(sp)
                proj_fm(ps[rows, :], wk, Lk, 512 * sp, 512)
                s.copy(KTt[rows, 512 * sp:512 * sp + 512], ps[rows, :], eng="dve")
            wv, Lv = load_w(l, f"v{kind}{gi}")
            for b in range(16):
                ps = zb(b % 4)[:, 0:Lv]
                for k in range(8):
                    s.mm(ps, hT[:, k, 1 + 128 * b:1 + 128 * b + 128], wv[:, k, 0:Lv], start=(k == 0), stop=(k == 7))
                if b % 2 == 0:
                    s.act(VT[:, 128 * b:128 * b + Lv], ps, AF.Identity)
                else:
                    s.copy(VT[:, 128 * b:128 * b + Lv], ps, eng="dve")
            sets = [(FP[0], FP[1], BP[2], BP[3]), (FP[2], FP[3], BP[5], BP[6])]
            if kind == "b":
                s.memset(FP[2][:, 0:1], 0.0)
            for hi, (h, pb) in enumerate(heads):
                g = (6 + h) if kind == "b" else (11 + h)
                m = g // 2
                assert pb == 64 * (g % 2)
                hr = slice(pb, pb + 64)
                vcol = 0 if (kind == "c" and gi == 0) else pb
                if kind == "b":
                    sb_head(sets, QT, KTt, VT, hr, vcol, m)
                else:
                    fox_head(QT, KTt, VT, hr, vcol, m, h)

        zctr = [0]

        def sb_head(sets, QT, KTt, VT, hr, vcol, m):
            A3 = [(BP[2], BP[3]), (BP[5], BP[6]), (BP[7], BP[8])]
            def S1(i):
                Qb, Sb, _a, _b = sets[i % 2]
                As, ATs = A3[i % 3]
                W = 128 * (i + 1)
                tq = slice(128 * i, 128 * i + 128)
                units = [(c, min(W, c + 1024)) for c in range(0, W, 1024)][::-1]
                fv = lambda t, a_, b_: V(t, t.ap[:, a_:b_], tuple(range(a_ // 256, (b_ - 1) // 256 + 1)))
                rv = lambda t, hi, lo: V(t, t.ap[:, hi:(lo if lo >= 0 else None):-1], tuple(range((lo + 1) // 256, hi // 256 + 1)))
                zss = []
                for (c0, c1) in units:
                    zs = 2 * (zctr[0] % 3)
                    zctr[0] += 1
                    zss.append(zs)
                    for b_ in range((c1 - c0 + 511) // 512):
                        k0 = c0 + 512 * b_
                        w_ = min(512, c1 - k0)
                        last = (k0 + w_ == W)
                        s.mm(bank(zs + b_)[:, 0:w_], QT[hr, tq], KTt[hr, k0:k0 + w_], start=True, stop=not last)
                        if last:
                            s.mm(bank(zs + b_)[:, w_ - 128:w_], cb("ident"), cb("nm_strict_ts"), start=False, stop=True)
                s.memset(fv(Qb, W, W + 1), 1.0, eng="pool")
                for ui, (c0, c1) in enumerate(units):
                    w = c1 - c0
                    zv = V(PS, PS.ap[:, 512 * zss[ui]:512 * zss[ui] + w], tuple(range(zss[ui], zss[ui] + (w + 511) // 512)))
                    s.act(fv(Qb, c0, c1), zv, AF.Sigmoid, scale=-1.0)
                    s.act(zv, zv, AF.Sigmoid)
                    init = 1.0 if ui == 0 else fv(Sb, c1, c1 + 1)
                    s.scan(rv(Sb, c1 - 1, c0 - 1), rv(Qb, c1, c0), cc("zero", 1).bc([128, w]), init, ALU.mult, ALU.add)
                    s.tt(As[:, c0:c1], zv, fv(Sb, c0, c1), ALU.mult)

            def S2T(i):
                As, ATs = A3[i % 3]
                for j in range(i + 1):
                    s.transpose(TP[:, 128 * (j % 8):128 * (j % 8) + 128], As[:, 128 * j:128 * j + 128], cb("ident"))
                    if j % 8 == 7 or j == i:
                        j0 = (j // 8) * 8
                        n_ = j - j0 + 1
                        s.act(ATs[:, 128 * j0:128 * j0 + 128 * n_], TP[:, 0:128 * n_], AF.Identity)

            def S2P(i):
                As, ATs = A3[i % 3]
                tq = slice(128 * i, 128 * i + 128)
                pO = P5[hr, 0:128]
                for j in range(i + 1):
                    s.mm(pO, VT[:, 128 * j + vcol:128 * j + vcol + 64], ATs[:, 128 * j:128 * j + 128], start=(j == 0), stop=(j == i))
                yv_ = V(yT, yT.ap[hr, m, tq], (m,))
                s.tt(yv_, pO, yv_, ALU.mult)

            for step in range(18):
                if step < 16:
                    S1(step)
                if 1 <= step < 17:
                    S2T(step - 1)
                if 2 <= step:
                    S2P(step - 2)

        def fox_head(QT, KTt, VT, hr, vcol, m, h):
            ncr = NEGC[0 if h < 3 else 1]
            nb = 32 * (h if h < 3 else h - 3)
            pb = hr.start
            ET, VA, QA, KA = BP[2], BP[3], BP[5], BP[6]
            s.copy(QA[0:64, :], QT[hr, :], eng="dve")
            s.act(KA[0:64, :], KTt[hr, :], AF.Identity)
            s.memset(QA[64:128, :], 0.0, eng="pool")
            s.memset(KA[64:128, :], 0.0, eng="pool")
            s.copy(QA[64:67, :], ncr[nb:nb + 3, :], eng="dve")
            s.memset(KA[64:67, :], -1.0)
            va3 = V(VA, VA.ap.rearrange("p (b c) -> p b c", c=128), (0, 1, 2, 3))
            vsrc = V(VT, VT.ap.rearrange("p (b c) -> p b c", c=128), (0, 1, 2, 3))
            s.copy(va3[:, :, pb:pb + 64], vsrc[:, :, vcol:vcol + 64], eng="dve")
            s.memset(va3[:, :, 64 - pb:128 - pb], 1.0)
            units = [(q, j) for q in range(2) for j in range(8 * q + 8)]
            dr = slice(64 - pb, 128 - pb)

            def halves(q, j):
                out_ = []
                for hh in range(2):
                    sp = 2 * q + hh
                    col0 = max(0, 128 * j - 512 * sp)
                    if col0 < 512:
                        out_.append((hh, sp, col0))
                return out_

            def A(u):
                q, j = units[u]
                zs = 2 * (u % 2)
                hv = halves(q, j)
                for hh, sp, col0 in hv:
                    t0 = 512 * sp
                    ps = bank(zs + hh)[:, col0:512]
                    diag = (128 * j >= t0)
                    s.mm(ps, KA[:, 128 * j:128 * j + 128], QA[:, t0 + col0:t0 + 512], start=True, stop=not diag)
                    if diag:
                        s.mm(bank(zs + hh)[:, col0:col0 + 128], cb("ident"), cb("nm_incl_st"), start=False, stop=True)
                h0, _sp0, c0 = hv[0]
                g0 = 512 * h0 + c0
                zin = V(PS, PS.ap[:, 512 * zs + g0:512 * zs + 1024], tuple(range(zs + h0, zs + 2)))
                e_ = u % 2
                eout = V(BP[2], BP[2].ap[:, 1024 * e_ + g0:1024 * e_ + 1024], tuple(range(2 * e_ + h0, 2 * e_ + 2)))
                s.act(eout, zin, AF.Exp, bias=NCT[:, j, h:h + 1])

            def B(u):
                q, j = units[u]
                e_ = u % 2
                for hh, sp, col0 in halves(q, j):
                    bk = 4 + 2 * (q % 2) + hh
                    et = V(BP[2], BP[2].ap[:, 1024 * e_ + 512 * hh + col0:1024 * e_ + 512 * hh + 512], (2 * e_ + hh,))
                    last = (j == 4 * sp + 3)
                    s.mm(bank(bk)[:, col0:512], va3[:, j, :], et, start=(j == 0), stop=last)
                    if last:
                        rc = V(FP[1], FP[1].ap[hr, 512 * hh:512 * hh + 512], tuple(range(8)))
                        s.recip(rc, bank(bk)[dr, :])
                        yv_ = V(yT, yT.ap[hr, m, 512 * sp:512 * sp + 512], (m,))
                        s.tt(rc, rc, yv_, ALU.mult)
                        s.tt(yv_, bank(bk)[hr, :], rc, ALU.mult)

            A(0)
            for u in range(1, len(units)):
                A(u)
                B(u - 1)
            B(len(units) - 1)

        def fox_prep(l):
            for ti, (nm, L) in enumerate((("fA", 96), ("fB", 64))):
                wv, _ = load_w(l, nm)
                rows = slice(0, L)
                cF = V(FP[0], FP[0].ap[rows, 0:2048], tuple(range(8)))
                s.ts(small[rows, 1:2], pc((l, "bfA" if ti == 0 else "bfB"), rows), -1.0, ALU.mult)
                for sp in range(4):
                    t0 = 512 * sp
                    ps = zb(sp)
                    proj_fm(ps[rows, :], wv, L, t0, 512)
                    e = V(FP[1], FP[1].ap[rows, 0:512], (0, 1))
                    s.act(e, ps[rows, :], AF.Exp, bias=small[rows, 1:2], scale=-1.0)
                    s.act(e, e, AF.Ln, bias=1.0)
                    init = 0.0 if sp == 0 else cF[:, t0 - 1:t0]
                    s.scan(cF[:, t0:t0 + 512], cc("ones", 512 // 4 if False else 128, rows)[:, 0:1].bc([L, 512]), e, init, ALU.mult, ALU.subtract)
                for b in range(16):
                    pt = zb(b % 4)[:, 0:L]
                    s.transpose(pt, cF[:, 128 * b:128 * b + 128], cc("ident", L, rows))
                    nh = 3 if ti == 0 else 2
                    s.copy(CT[:, b, 3 * ti:3 * ti + nh], pt[:, 0:32 * nh:32], eng="dve")
                hi = V(BP[5], BP[5].ap[rows, :], (0, 1, 2, 3))
                lo = V(BP[6], BP[6].ap[rows, :], (0, 1, 2, 3))
                r1 = V(FP[1], FP[1].ap[rows, 0:2048], tuple(range(8)))
                ng = NEGC[ti][rows, :]
                s.ts(hi, cF, -1.0, ALU.mult)
                s.stt(r1, cF, -1.0, hi, ALU.mult, ALU.subtract)
                s.copy(lo, r1, eng="dve")
                s.ts(ng, hi, cc("sel", 1, rows), ALU.mult)
                s.tt(r1, r1, lo, ALU.subtract)
                s.stt(ng, lo, cst[rows, CC["sel"] + 1:CC["sel"] + 2], ng, ALU.mult, ALU.add)
                s.copy(hi, r1, eng="dve")
                s.stt(ng, hi, cst[rows, CC["sel"] + 2:CC["sel"] + 3], ng, ALU.mult, ALU.add)
            s.memset(NCT.v(), 0.0)
            s.ts(NCT[:, :, 0:5], CT[:, :, 0:5], -1.0, ALU.mult)

        def out_proj(l, src, dst, nwkeys, last):
            o, L = GOFF["wo"]
            for t4 in range(4):
                s.dma(BP[t4].v(), V(wpack, wpack.ap[l, :, o + 2048 * t4:o + 2048 * t4 + 2048], (0,)), q="pool")
            if "a" in parts and not last:
                prefetch_layer(l + 1, False)
            wo = lambda k, j: BP[k // 2][:, 1024 * (k % 2) + 128 * j:1024 * (k % 2) + 128 * j + 128]
            srcv = src.ap.rearrange("(k p) t -> p k t", p=128)
            dstv = dst.ap.rearrange("(k p) t -> p k t", p=128)
            fin = []
            def issue_loads(sp_):
                for v_, k0, nk in xset(sp_):
                    s.dma(v_, V(src, srcv[:, k0:k0 + nk, 512 * sp_:512 * sp_ + 512], tuple(range(k0, k0 + nk))))
            issue_loads(0)
            for sp in range(4):
                t0 = 512 * sp
                pcs = xset(sp)
                if sp + 1 < 4:
                    issue_loads(sp + 1)
                sq = [V(BP[7 + i], BP[7 + i].ap.rearrange("p (k t) -> p k t", k=4), (0, 1, 2, 3)) for i in range(2)]
                for j in range(8):
                    ps = zb(j % 4)
                    for k in range(8):
                        s.mm(ps, wo(k, j), V(yT, yT.ap[:, k, t0:t0 + 512], (k,)), start=(k == 0), stop=(k == 7))
                    xj = xk(pcs, j)
                    s.tt(xj, ps, xj, ALU.add)
                    s.act(sq[j // 4][:, j % 4, :], xj, AF.Square)
                if not last:
                    for v_, k0, nk in pcs:
                        s.dma(V(dst, dstv[:, k0:k0 + nk, t0:t0 + 512], tuple(range(k0, k0 + nk))), v_)
                for k in range(8):
                    s.mm(P5.v(), cb("ones"), sq[k // 4][:, k % 4, :], start=(k == 0), stop=(k == 7))
                s.act(P6.v(), P5.v(), AF.Ln, bias=NEPS.v(), scale=1.0 / DM)
                s.act(P7.v(), P6.v(), AF.Exp, scale=-0.5)
                for k in range(8):
                    xk_ = xk(pcs, k)
                    if not last:
                        s.stt(hT[:, k, 1 + t0:1 + t0 + 512], xk_, pc(nwkeys(k)), P7.v(), ALU.mult, ALU.mult)
                    else:
                        s.stt(xk_, xk_, pc(nwkeys(k)), P7.v(), ALU.mult, ALU.mult)
                if last:
                    ov = outT.ap.rearrange("(k p) t -> p k t", p=128)
                    for v_, k0, nk in pcs:
                        fin.append(s.dma(V(outT, ov[:, k0:k0 + nk, t0:t0 + 512], tuple(range(k0, k0 + nk))), v_))
            return fin

        srcs = [xT, xs1, xs2]
        fin = []
        for l in range(nlayers):
            if l == 0:
                if "a" in parts:
                    prefetch_layer(0, True)
                norm_phase(srcs[l], lambda k, l=l: (l, "nw", k), True)
            gates_all(l)
            if "a" in parts:
                rwkv(l)
            else:
                for m in range(3):
                    s.memset(V(yT, yT.ap[:, m, :], (m,)), 0.0, eng="pool")
            if "b" in parts:
                for gi in range(3):
                    attn_group(l, "b", gi)
            if "c" in parts:
                fox_prep(l)
                for gi in range(3):
                    attn_group(l, "c", gi)
            if debug == l + 1:
                f = s.dma(dbg.v(), V(yT, yT.ap.rearrange("p k t -> p (k t)"), tuple(range(8))))
                print("deadlock check:", s.check())
                s.emit(final_waits=[f])
                return nc, s.stats()
            last = (l == nlayers - 1)
            fin = out_proj(l, srcs[l], srcs[l + 1], (lambda k: ("fnw", k)) if last else (lambda k, l=l: (l + 1, "nw", k)), last)
        s.emit(final_waits=fin)
        stats = s.stats()
    return nc, stats


_CACHE = {}


def kernel(**inputs):
    inp = {k: np.asarray(v) for k, v in inputs.items()}
    wpack, pcol, lora = _host_pack(inp)
    consts = _host_consts()
    if "nc" not in _CACHE:
        _CACHE["nc"] = build()[0]
    nc = _CACHE["nc"]
    x = inp["x"]
    in_maps = []
    for b in range(8):
        in_maps.append({"xT": np.ascontiguousarray(x[b].T), "wpack": wpack, "pcol": pcol,
                        "mu": np.ascontiguousarray(inp["mu"]), "lora": lora, "consts": consts})
    res = run_bass_kernel_spmd(nc, in_maps, core_ids=list(range(8)))
    out = np.stack([np.ascontiguousarray(r["outT"].T) for r in res.results]).astype(np.float32)
    return out
```
